# Optimizing a Trainium2 kernel written in Bass

```python
import math
import jax, jax.numpy as jnp
from jax import lax
import numpy as np

D_MODEL = 1024
BATCH = 8
SEQ = 2048
DEPTH = 1

RET_HEADS = 4
RET_QK_DIM = D_MODEL // RET_HEADS
RET_V_DIM = 2 * RET_QK_DIM
RET_CHUNK = 128
DIFF_HEAD_DIM = 64
DIFF_HEADS = D_MODEL // (2 * DIFF_HEAD_DIM)
DIFF_V_DIM = 2 * DIFF_HEAD_DIM
Q_BLOCK = 128
D_FF = 2816
N_BRANCH = 2
N_SUB = 3
N_MOD = 3
EPS = 1e-6

RET_QK_W = RET_HEADS * RET_QK_DIM
RET_V_W = RET_HEADS * RET_V_DIM
DIFF_QK_W = DIFF_HEADS * 2 * DIFF_HEAD_DIM
DIFF_V_W = DIFF_HEADS * DIFF_V_DIM
GATE_W = N_BRANCH * D_MODEL
IN_W = 2 * RET_QK_W + 2 * RET_V_W + 2 * DIFF_QK_W + DIFF_V_W + GATE_W
SPLIT_POINTS = (
    RET_QK_W,
    2 * RET_QK_W,
    2 * RET_QK_W + RET_V_W,
    2 * RET_QK_W + 2 * RET_V_W,
    2 * RET_QK_W + 2 * RET_V_W + DIFF_QK_W,
    2 * RET_QK_W + 2 * RET_V_W + 2 * DIFF_QK_W,
    2 * RET_QK_W + 2 * RET_V_W + 2 * DIFF_QK_W + DIFF_V_W,
)

kernel_name = "hybrid_retention_diffattn_macaron_adaln"


def lambda_init(layer_idx):
    return 0.8 - 0.6 * math.exp(-0.3 * layer_idx)


def rmsnorm(x, g):
    xf = x.astype(jnp.float32)
    y = xf * lax.rsqrt(jnp.mean(xf * xf, axis=-1, keepdims=True) + EPS)
    return (y * g.astype(jnp.float32)).astype(x.dtype)


def modulate(x, g, shift, scale):
    return rmsnorm(x, g) * (1 + scale[:, None, :]) + shift[:, None, :]


def swiglu(h, w_in, w_out):
    a, b = jnp.split(h @ w_in, 2, axis=-1)
    return (jax.nn.silu(a) * b) @ w_out


def retention_chunkwise(q, k, v):
    B, S, H, dk = q.shape
    dv = v.shape[-1]
    C = RET_CHUNK
    N = S // C
    log_g = jnp.log1p(-jnp.exp2(-5.0 - jnp.arange(H, dtype=jnp.float32)))
    pos = jnp.arange(C, dtype=jnp.float32)
    rel = pos[:, None] - pos[None, :]
    decay_intra = jnp.where(rel[None] >= 0,
                            jnp.exp(jnp.maximum(rel, 0.0)[None] * log_g[:, None, None]),
                            0.0)
    xi = jnp.exp((pos + 1.0)[None, :] * log_g[:, None])[..., None]
    zeta = jnp.exp((C - 1.0 - pos)[None, :] * log_g[:, None])[..., None]
    g_chunk = jnp.exp(C * log_g)[:, None, None]

    def to_chunks(t):
        return t.reshape(B, N, C, H, t.shape[-1]).transpose(1, 0, 3, 2, 4)

    def step(R, inp):
        qc, kc, vc = inp
        intra = jnp.einsum('bhcd,bhsd->bhcs', qc, kc) * decay_intra
        o = (jnp.einsum('bhcs,bhse->bhce', intra, vc)
             + jnp.einsum('bhcd,bhde->bhce', qc, R) * xi)
        R = R * g_chunk + jnp.einsum('bhsd,bhse->bhde', kc * zeta, vc)
        return R, o

    R0 = jnp.zeros((B, H, dk, dv), jnp.float32)
    _, o = lax.scan(step, R0, (to_chunks(q), to_chunks(k), to_chunks(v)))
    return o.transpose(1, 0, 3, 2, 4).reshape(B, S, H, dv)


def diff_attention(q, k, v, lam):
    B, H, _, S, dh = q.shape
    dv = v.shape[-1]
    nb = S // Q_BLOCK
    scale = dh ** -0.5
    slopes = jnp.exp2(-8.0 * (jnp.arange(H, dtype=jnp.float32) + 1.0) / H)
    kpos = jnp.arange(S)

    def block(i):
        start = i * Q_BLOCK
        qb = lax.dynamic_slice_in_dim(q, start, Q_BLOCK, axis=3)
        qpos = start + jnp.arange(Q_BLOCK)
        dist = (qpos[:, None] - kpos[None, :]).astype(jnp.float32)
        bias = -slopes[:, None, None] * dist
        s = jnp.einsum('bhmqd,bhmkd->bhmqk', qb, k) * scale + bias[None, :, None]
        s = jnp.where(dist >= 0, s, -jnp.inf)
        p = jax.nn.softmax(s, axis=-1)
        a = p[:, :, 0] - lam * p[:, :, 1]
        return jnp.einsum('bhqk,bhkd->bhqd', a, v)

    o = lax.map(block, jnp.arange(nb))
    return o.transpose(1, 0, 3, 2, 4).reshape(B, S, H, dv)


def setup_inputs(seed: int = 0) -> dict:
    key = jax.random.key(seed)
    ks = jax.random.split(key, 18)
    L, D, F = DEPTH, D_MODEL, D_FF
    nrm = lambda k, shape, fan_in: jax.random.normal(k, shape, jnp.float32) * fan_in ** -0.5
    return {
        "x": jax.random.normal(ks[0], (BATCH, SEQ, D), jnp.float32),
        "c": jax.random.normal(ks[1], (BATCH, D), jnp.float32),
        "w_cond": nrm(ks[2], (L, D, N_SUB * N_MOD * D), D),
        "b_cond": 0.01 * jax.random.normal(ks[3], (L, N_SUB * N_MOD * D), jnp.float32),
        "g_norm": 1.0 + 0.02 * jax.random.normal(ks[4], (L, N_SUB, D), jnp.float32),
        "w_ffn1_in": nrm(ks[5], (L, D, 2 * F), D),
        "w_ffn1_out": nrm(ks[6], (L, F, D), F),
        "w_in": nrm(ks[7], (L, D, IN_W), D),
        "w_ret_out": nrm(ks[8], (L, RET_V_W, D), RET_V_W),
        "diff_lambda": 0.1 * jax.random.normal(ks[9], (L, 4, DIFF_HEAD_DIM), jnp.float32),
        "diff_subln": 1.0 + 0.02 * jax.random.normal(ks[10], (L, DIFF_V_DIM), jnp.float32),
        "w_diff_out": nrm(ks[11], (L, DIFF_V_W, D), DIFF_V_W),
        "w_out": nrm(ks[12], (L, D, D), D),
        "w_ffn2_in": nrm(ks[13], (L, D, 2 * F), D),
        "w_ffn2_out": nrm(ks[14], (L, F, D), F),
        "g_final": 1.0 + 0.02 * jax.random.normal(ks[15], (D,), jnp.float32),
    }


def reference(x, c, w_cond, b_cond, g_norm, w_ffn1_in, w_ffn1_out, w_in, w_ret_out,
              diff_lambda, diff_subln, w_diff_out, w_out, w_ffn2_in, w_ffn2_out, g_final):
    B, S, D = x.shape
    c_act = jax.nn.silu(c)
    for l in range(DEPTH):
        mod = (c_act @ w_cond[l] + b_cond[l]).reshape(B, N_SUB, N_MOD, D)

        h = modulate(x, g_norm[l, 0], mod[:, 0, 0], mod[:, 0, 1])
        x = x + 0.5 * mod[:, 0, 2][:, None, :] * swiglu(h, w_ffn1_in[l], w_ffn1_out[l])

        h = modulate(x, g_norm[l, 1], mod[:, 1, 0], mod[:, 1, 1])
        rq, rk, rv, rg, dq, dk_, dv_, gates = jnp.split(h @ w_in[l], SPLIT_POINTS, axis=-1)

        f32 = jnp.float32
        rq = rq.reshape(B, S, RET_HEADS, RET_QK_DIM).astype(f32)
        rk = rk.reshape(B, S, RET_HEADS, RET_QK_DIM).astype(f32) * RET_QK_DIM ** -0.5
        rv = rv.reshape(B, S, RET_HEADS, RET_V_DIM).astype(f32)
        ro = retention_chunkwise(rq, rk, rv)
        mu = jnp.mean(ro, axis=-1, keepdims=True)
        var = jnp.mean(jnp.square(ro - mu), axis=-1, keepdims=True)
        ro = ((ro - mu) * lax.rsqrt(var + EPS)).reshape(B, S, RET_V_W).astype(x.dtype)
        y_ret = (jax.nn.silu(rg) * ro) @ w_ret_out[l]

        lam_init = lambda_init(l)
        lp = diff_lambda[l].astype(f32)
        lam = jnp.exp(jnp.sum(lp[0] * lp[1])) - jnp.exp(jnp.sum(lp[2] * lp[3])) + lam_init
        dq = dq.reshape(B, S, DIFF_HEADS, 2, DIFF_HEAD_DIM).transpose(0, 2, 3, 1, 4).astype(f32)
        dk_ = dk_.reshape(B, S, DIFF_HEADS, 2, DIFF_HEAD_DIM).transpose(0, 2, 3, 1, 4).astype(f32)
        dv_ = dv_.reshape(B, S, DIFF_HEADS, DIFF_V_DIM).transpose(0, 2, 1, 3).astype(f32)
        do = diff_attention(dq, dk_, dv_, lam)
        do = rmsnorm(do, diff_subln[l]) * (1.0 - lam_init)
        y_diff = do.reshape(B, S, DIFF_V_W).astype(x.dtype) @ w_diff_out[l]

        gates = jax.nn.sigmoid(gates.reshape(B, S, N_BRANCH, D))
        y = gates[:, :, 0] * y_ret + gates[:, :, 1] * y_diff
        x = x + mod[:, 1, 2][:, None, :] * (y @ w_out[l])

        h = modulate(x, g_norm[l, 2], mod[:, 2, 0], mod[:, 2, 1])
        x = x + 0.5 * mod[:, 2, 2][:, None, :] * swiglu(h, w_ffn2_in[l], w_ffn2_out[l])

    return rmsnorm(x, g_final)
```

```python
import contextlib
import numpy as np
import ml_dtypes
import concourse.bass as bass
import concourse.mybir as mybir
from concourse.bass_utils import run_bass_kernel_spmd

dt = mybir.dt
AF = mybir.ActivationFunctionType
ALU = mybir.AluOpType
F32, BF16 = dt.float32, dt.bfloat16
ISZ = {dt.float32: 4, dt.bfloat16: 2, dt.int32: 4, dt.uint32: 4, dt.float16: 2}

D = 1024
S = 2048
DFF = 2816
KC = D // 128
NT = S // 128
NB = S // 512
EPS = 1e-6
O_RQ, O_RK, O_RV, O_RG, O_DQ, O_DK, O_DV, O_G0, O_G1 = 0, 1024, 2048, 4096, 6144, 7168, 8192, 9216, 10240
LAM_INIT = 0.8 - 0.6 * float(np.exp(-0.3 * 0))
NEG = -30000.0

ENGS = ("pe", "act", "dve", "pool", "sp")
SB_G = 128


class Prog:
    def __init__(self, nc):
        self.nc = nc
        self.ops = []
        self.sb_off = (int(nc.sbuf_base) + 255) // 256 * 256
        self.sb_top = int(nc.sbuf_top)
        self.tinfo = {}
        self.psum_banks = 0
        self.cells = {}

    def sb(self, name, shape, dtype, at=None):
        isz = ISZ[dtype]
        free = int(np.prod(shape[1:]))
        nbytes = free * isz
        if at is None:
            at = self.sb_off
            self.sb_off = (at + nbytes + 255) // 256 * 256
        assert at + nbytes <= self.sb_top, f"SBUF overflow at {name}: {at + nbytes} > {self.sb_top}"
        h = self.nc.alloc_sbuf_tensor_at(name, list(shape), dtype, offset=at)
        self.tinfo[h.name] = ("sb", at, free, isz)
        return h

    def ps(self, name):
        h = self.nc.alloc_psum_tensor(name, [128, 512], F32)
        self.tinfo[h.name] = ("ps", self.psum_banks, 512, 4)
        self.psum_banks += 1
        return h

    def cells_of(self, ap):
        name = ap.tensor.name
        info = self.tinfo.get(name)
        if info is None:
            return []
        space, base, free, isz0 = info
        if space == "ps":
            if free * isz0 <= 2048:
                return [("ps", base)]
            isz_ = ISZ[ap.dtype]
            pstr = free * isz0 // isz_
            e0_ = int(ap.offset) % pstr
            ext_ = 0
            for st, cnt in ap.ap[1:]:
                ext_ += (cnt - 1) * abs(st)
            lo_ = e0_ * isz_
            hi_ = (e0_ + ext_ + 1) * isz_
            return [("ps", base + b) for b in range(lo_ // 2048, (hi_ - 1) // 2048 + 1)]
        isz = ISZ[ap.dtype]
        pstride = free * isz0 // isz
        off = int(ap.offset)
        p0 = off // pstride
        e0 = off % pstride
        dims = ap.ap
        npart = dims[0][1]
        ext = 0
        for st, cnt in dims[1:]:
            ext += (cnt - 1) * abs(st)
        lo = base + e0 * isz
        hi = base + (e0 + ext + 1) * isz
        out = []
        for q in range(p0 // 32, (p0 + npart - 1) // 32 + 1):
            for c in range(lo // SB_G, (hi - 1) // SB_G + 1):
                out.append(("sb", q, c))
        return out

    def op(self, eng, fn, reads=(), writes=(), dma=None, dreads=(), dwrites=()):
        idx = len(self.ops)
        deps = set()
        rc, wc = [], []
        for a in reads:
            rc.extend(self.cells_of(a))
        for a in writes:
            wc.extend(self.cells_of(a))
        rc.extend(("dram", k) for k in dreads)
        wc.extend(("dram", k) for k in dwrites)
        ops = self.ops
        for c in rc:
            st = self.cells.get(c)
            if st is not None:
                if st[0] is not None:
                    deps.add(st[0])
                if c[0] == "ps":
                    for r in st[1].values():
                        if ops[r]["eng"] != eng:
                            deps.add(r)
        for c in wc:
            st = self.cells.get(c)
            if st is not None:
                if st[0] is not None:
                    deps.add(st[0])
                deps.update(st[1].values())
        rkey = ("d", idx) if dma is not None else eng
        for c in rc:
            st = self.cells.get(c)
            if st is None:
                st = [None, {}]
                self.cells[c] = st
            st[1][rkey] = idx
        for c in wc:
            st = self.cells.get(c)
            if st is None:
                st = [None, {}]
                self.cells[c] = st
            st[0] = idx
            st[1] = {}
        deps.discard(idx)
        self.ops.append(dict(eng=eng, fn=fn, deps=deps, dma=dma))
        return idx

    def emit(self):
        nc = self.nc
        ops = self.ops
        needs = [False] * len(ops)
        for o in ops:
            keep = set()
            latest = {}
            for d in o["deps"]:
                p = ops[d]
                if p["dma"] is not None:
                    keep.add(d)
                    continue
                if p["eng"] == "pe" and o["eng"] == "pe":
                    continue
                if d > latest.get(p["eng"], -1):
                    latest[p["eng"]] = d
            keep.update(latest.values())
            o["deps"] = keep
            for d in keep:
                needs[d] = True
        eng_cnt = {e: 0 for e in ENGS}
        dma_cnt = {}
        sig = [None] * len(ops)
        for i, o in enumerate(ops):
            if o["dma"] is not None:
                k = ("dma", o["dma"])
                dma_cnt[k] = dma_cnt.get(k, 0) + 16
                sig[i] = (k, dma_cnt[k])
            elif needs[i]:
                eng_cnt[o["eng"]] += 1
                sig[i] = (("eng", o["eng"]), eng_cnt[o["eng"]])
        semkeys = [("eng", e) for e in ENGS] + sorted(dma_cnt.keys())
        per_eng = {e: [] for e in ENGS}
        for i, o in enumerate(ops):
            per_eng[o["eng"]].append(i)
        self.stats = dict(n_ops=len(ops), per_eng={e: len(v) for e, v in per_eng.items()},
                          eng_cnt=dict(eng_cnt), n_sems=len(semkeys), waits={})
        with contextlib.ExitStack() as es:
            sems = {}
            for k in semkeys:
                sems[k] = es.enter_context(nc.semaphore("s_" + "_".join(str(x) for x in k)))
            block = es.enter_context(nc.Block())

            def run(engname, eng):
                waited = {}
                nwait = 0
                for i in per_eng[engname]:
                    o = ops[i]
                    req = {}
                    for d in o["deps"]:
                        k, v = sig[d]
                        if v > req.get(k, 0):
                            req[k] = v
                    for k, v in req.items():
                        if waited.get(k, 0) >= v:
                            continue
                        eng.wait_ge(sems[k], v)
                        waited[k] = v
                        nwait += 1
                    ins = o["fn"](eng)
                    if sig[i] is not None:
                        k, v = sig[i]
                        ins.then_inc(sems[k], 16 if k[0] == "dma" else 1)
                self.stats["waits"][engname] = nwait

            @block.tensor
            def _(e):
                run("pe", e)

            @block.scalar
            def _(e):
                run("act", e)

            @block.vector
            def _(e):
                run("dve", e)

            @block.gpsimd
            def _(e):
                run("pool", e)

            @block.sync
            def _(e):
                run("sp", e)


class K:
    def __init__(self, nc, stage=None):
        self.nc = nc
        self.stage = stage
        self.P = Prog(nc)
        self.rr = {}

    def mm(self, out, lhsT, rhs, start, stop, skip=False):
        if skip:
            self.P.op("pe", lambda e: e.matmul(out, lhsT=lhsT, rhs=rhs, start=start, stop=stop,
                                               skip_group_check=True),
                      reads=[lhsT, rhs], writes=[out])
        else:
            self.P.op("pe", lambda e: e.matmul(out, lhsT=lhsT, rhs=rhs, start=start, stop=stop),
                      reads=[lhsT, rhs], writes=[out])

    def tr(self, out, in_, ident):
        self.P.op("pe", lambda e: e.transpose(out=out, in_=in_, identity=ident),
                  reads=[in_, ident], writes=[out])

    def act(self, out, in_, func, scale=1.0, bias=None, accum=None):
        reads = [in_]
        kw = {}
        if not isinstance(scale, (int, float)):
            reads.append(scale)
        if bias is not None:
            kw["bias"] = bias
            if not isinstance(bias, (int, float)):
                reads.append(bias)
        writes = [out]
        if accum is not None:
            kw["accum_out"] = accum
            writes.append(accum)
        self.P.op("act", lambda e: e.activation(out=out, in_=in_, func=func, scale=scale, **kw),
                  reads=reads, writes=writes)

    def tt(self, eng, out, in0, in1, op):
        self.P.op(eng, lambda e: e.tensor_tensor(out=out, in0=in0, in1=in1, op=op),
                  reads=[in0, in1], writes=[out])

    def ts(self, eng, out, in0, s1, op0, s2=None, op1=None):
        reads = [in0] + [s for s in (s1, s2) if s is not None and not isinstance(s, (int, float))]
        if op1 is None:
            self.P.op(eng, lambda e: e.tensor_scalar(out=out, in0=in0, scalar1=s1, scalar2=None, op0=op0),
                      reads=reads, writes=[out])
        else:
            self.P.op(eng, lambda e: e.tensor_scalar(out=out, in0=in0, scalar1=s1, scalar2=s2, op0=op0, op1=op1),
                      reads=reads, writes=[out])

    def stt(self, eng, out, in0, scalar, in1, op0, op1):
        reads = [in0, in1] + ([] if isinstance(scalar, (int, float)) else [scalar])
        self.P.op(eng, lambda e: e.scalar_tensor_tensor(out=out, in0=in0, scalar=scalar, in1=in1, op0=op0, op1=op1),
                  reads=reads, writes=[out])

    def cp(self, eng, out, in_):
        if eng == "act":
            self.act(out, in_, AF.Copy)
        else:
            self.P.op(eng, lambda e: e.tensor_copy(out=out, in_=in_), reads=[in_], writes=[out])

    def memset(self, eng, ap, val):
        self.P.op(eng, lambda e: e.memset(ap, val), writes=[ap])

    def dma(self, q, out, in_, sem, reads=(), writes=(), dreads=(), dwrites=()):
        self.P.op(q, lambda e: e.dma_start(out=out, in_=in_), reads=list(reads), writes=list(writes),
                  dma=sem, dreads=dreads, dwrites=dwrites)

    def rot(self, key, n):
        v = self.rr.get(key, 0)
        self.rr[key] = v + 1
        return v % n

    def wload(self, ring, parts, after=()):
        slots = self.wA if ring == "A" else self.wB
        i = self.rot("w" + ring, len(slots))
        slot = slots[i]
        off = 0
        views = []
        for pi, (src, kc, ncols) in enumerate(parts):
            n = kc * ncols
            v = slot[:, off:off + n].rearrange("p (k c) -> p k c", k=kc)
            self.dma("pool", v, src.rearrange("(k p) c -> p k c", p=128), f"w{ring}{i}_{pi}", writes=[v],
                     dreads=after)
            views.append(v)
            off += n
        assert off <= slot.shape[1]
        return views

    def build(self):
        nc, P = self.nc, self.P
        stage = self.stage
        din = lambda name, shape: nc.dram_tensor(name, list(shape), F32, kind="ExternalInput").ap()
        x_d = din("x", [S, D])
        c_d = din("c", [128, KC])
        wcond_d = din("w_cond", [D, 9 * D])
        bcond_d = din("b_cond", [128, 72])
        gnorm_d = din("g_norm", [128, 24])
        wf_in = [din("w_ffn1_in", [D, 2 * DFF]), din("w_ffn2_in", [D, 2 * DFF])]
        wf_out = [din("w_ffn1_out", [DFF, D]), din("w_ffn2_out", [DFF, D])]
        win_d = din("w_in", [D, 11264])
        wro_d = din("w_ret_out", [2048, D])
        lam_d = din("diff_lambda", [256])
        subln_d = din("diff_subln", [128])
        wdo_d = din("w_diff_out", [D, D])
        wout_d = din("w_out", [D, D])
        gfin_d = din("g_final", [128, KC])
        ident_d = din("c_ident", [128, 128])
        kdec_d = din("c_kdec", [4, 128, 128])
        xi_d = din("c_xi", [128, 4])
        m01_d = din("c_mask01", [128, 128])
        um_d = din("c_umask", [128, 128])
        qx_d = nc.dram_tensor("c_qx", [8, 6, S], BF16, kind="ExternalInput").ap()
        kx_d = nc.dram_tensor("c_kx", [8, 6, S], BF16, kind="ExternalInput").ap()
        out_d = nc.dram_tensor("out", [S, D], F32, kind="ExternalOutput").ap()
        spk = "ExternalOutput" if stage in ("ret", "diff") else "Internal"
        ret_sp = nc.dram_tensor("ret_sp", [8, 128, 16, 256], BF16, kind=spk).ap()
        diff_sp = nc.dram_tensor("diff_sp", [8, 128, 8, 256], BF16, kind=spk).ap()
        if stage is not None:
            dbg_x = nc.dram_tensor("dbg_x", [128, KC, S], F32, kind="ExternalOutput").ap()
            dbg_h = nc.dram_tensor("dbg_h", [128, KC, S], BF16, kind="ExternalOutput").ap()
            dbg_m = nc.dram_tensor("dbg_m", [128, 80], F32, kind="ExternalOutput").ap()
        self.x_d, self.out_d = x_d, out_d

        self.xT = xT = P.sb("xT", [128, KC, S], F32)
        self.hT = hT = P.sb("hT", [128, KC, S], BF16)
        self.wA = [P.sb(f"wA{i}", [128, 8192], BF16) for i in range(2)]
        self.wB = [P.sb(f"wB{i}", [128, 4096], BF16) for i in range(2)]
        id32 = P.sb("id32", [128, 128], F32)
        idb = P.sb("idb", [128, 128], BF16)
        ones32 = P.sb("ones32", [128, 128], F32)
        nhalf = P.sb("nhalf", [128, 8], F32)
        m01 = P.sb("m01", [128, 128], F32)
        umask = P.sb("umask", [128, 128], BF16)
        sublnB = P.sb("sublnB", [128, 128], F32)
        kdec = P.sb("kdec", [128, 4, 128], F32)
        xi = P.sb("xi", [128, 4], F32)
        c_sb = P.sb("c_sb", [128, KC], F32)
        c_bf = P.sb("c_bf", [128, KC], BF16)
        self._c_bf = c_bf
        bcond = P.sb("bcond", [128, 72], F32)
        gnorm = P.sb("gnorm", [128, 24], F32)
        gfin = P.sb("gfin", [128, KC], F32)
        modT = P.sb("modT", [128, 72], F32)
        gs = P.sb("gs", [128, 24], F32)
        gatef = P.sb("gatef", [128, 24], F32)
        lamb = P.sb("lamb", [128, 256], F32)
        lsm = P.sb("lsm", [128, 72], F32)
        neglam = P.sb("neglam", [128, 1], F32)
        self.id32, self.idb, self.ones32, self.nhalf = id32, idb, ones32, nhalf
        self.gs, self.gatef, self.modT = gs, gatef, modT
        arena = P.sb_off
        self.arena = arena
        asz = P.sb_top - arena
        self.asz = asz
        ps_lo = [P.ps(f"ps{i}") for i in range(2)]
        self.psbig = nc.alloc_psum_tensor("psbig", [128, 2048], F32)
        P.tinfo[self.psbig.name] = ("ps", 2, 2048, 4)
        P.psum_banks += 4
        ps_hi = [P.ps(f"ps{i}") for i in (6, 7)]
        self.ps = ps_lo + [self.psbig[:, 512 * j:512 * (j + 1)] for j in range(4)] + ps_hi
        ps = self.ps

        sp = "sp"
        self.dma(sp, id32[:, :], ident_d, "c0", writes=[id32[:, :]])
        self.dma("pool", idb[:, :], ident_d, "c1", writes=[idb[:, :]])
        self.dma(sp, m01[:, :], m01_d, "c2", writes=[m01[:, :]])
        self.dma("pool", umask[:, :], um_d, "c3", writes=[umask[:, :]])
        self.dma(sp, kdec[:, :, :], kdec_d.rearrange("h p t -> p h t"), "c4", writes=[kdec[:, :, :]])
        self.dma(sp, xi[:, :], xi_d, "c5", writes=[xi[:, :]])
        self.dma(sp, c_sb[:, :], c_d, "c6", writes=[c_sb[:, :]])
        self.dma(sp, bcond[:, :], bcond_d, "c7", writes=[bcond[:, :]])
        self.dma(sp, gnorm[:, :], gnorm_d, "c8", writes=[gnorm[:, :]])
        self.dma(sp, gfin[:, :], gfin_d, "c9", writes=[gfin[:, :]])
        self.dma(sp, lamb[:, :], lam_d.partition_broadcast(128), "c10", writes=[lamb[:, :]])
        self.dma(sp, sublnB[:, :], subln_d.partition_broadcast(128), "c11", writes=[sublnB[:, :]])
        self.memset("dve", ones32[:, :], 1.0)
        self.memset("dve", nhalf[:, :], -0.5)
        self.ts("dve", sublnB[:, :], sublnB[:, :], 1.0 - LAM_INIT, ALU.mult)
        self.act(c_bf[:, :], c_sb[:, :], AF.Silu)
        self.tt("dve", lsm[:, 0:64], lamb[:, 0:64], lamb[:, 64:128], ALU.mult)
        P.op("dve", lambda e: e.reduce_sum(out=lsm[:, 64:65], in_=lsm[:, 0:64], axis=mybir.AxisListType.X),
             reads=[lsm[:, 0:64]], writes=[lsm[:, 64:65]])
        self.tt("dve", lsm[:, 0:64], lamb[:, 128:192], lamb[:, 192:256], ALU.mult)
        P.op("dve", lambda e: e.reduce_sum(out=lsm[:, 65:66], in_=lsm[:, 0:64], axis=mybir.AxisListType.X),
             reads=[lsm[:, 0:64]], writes=[lsm[:, 65:66]])
        self.act(lsm[:, 66:68], lsm[:, 64:66], AF.Exp)
        self.tt("dve", lsm[:, 68:69], lsm[:, 67:68], lsm[:, 66:67], ALU.subtract)
        self.ts("dve", neglam[:, :], lsm[:, 68:69], -LAM_INIT, ALU.add)

        xs = [P.sb(f"xs{i}", [128, D], F32, at=arena + i * 4096) for i in range(4)]
        for tt_ in range(NT):
            b = tt_ % 4
            self.dma(sp, xs[b][:, :], x_d[tt_ * 128:(tt_ + 1) * 128, :], f"xin{b}", writes=[xs[b][:, :]],
                     dwrites=[("xk", tt_)])
            for half in range(2):
                bank = ps[self.rot("psx", 4)]
                for j in range(4):
                    dc = half * 4 + j
                    self.tr(bank[:, j * 128:(j + 1) * 128], xs[b][:, dc * 128:(dc + 1) * 128], id32[:, :])
                dst = xT[:, half * 4:half * 4 + 4, tt_ * 128:(tt_ + 1) * 128]
                src = bank[:, :].rearrange("p (a b) -> p a b", a=4)
                self.cp("act" if half == 0 else "dve", dst, src)

        self.wC = P.sb("wC", [128, 4096], BF16, at=arena + 34816)
        self.norm_stats(0)
        self.mod_groups(wcond_d, bcond, gnorm, 0)
        self.norm_apply(0)
        if stage == "norm0":
            return self.finish_debug(dbg_x, dbg_h, dbg_m)
        halves = [(g, hf) for g in range(3, 9) for hf in range(2)]

        def extra(gi):
            for (g, hf) in halves[2 * gi:2 * gi + 2]:
                self.mod_half(wcond_d, g, hf)
        def extra2(gi):
            extra(gi)
            if gi == 5:
                self.mod_finish(bcond, gnorm, 1)
                self.mod_finish(bcond, gnorm, 2)
        self.ffn(0, wf_in[0], wf_out[0], extra=extra2, tail=lambda tb: self.norm_block(1, tb))
        if stage == "ffn1":
            return self.finish_debug(dbg_x, dbg_h, dbg_m)
        self.retention(win_d, kdec, xi, m01, ret_sp)
        if stage == "ret":
            return self.finish_debug(dbg_x, dbg_h, dbg_m)
        self.diffattn(win_d, qx_d, kx_d, umask, sublnB, neglam, diff_sp)
        if stage == "diff":
            return self.finish_debug(dbg_x, dbg_h, dbg_m)
        self.merge(win_d, wro_d, wdo_d, wout_d, ret_sp, diff_sp, tail=lambda tb: self.norm_block(2, tb))
        if stage == "mix":
            return self.finish_debug(dbg_x, dbg_h, dbg_m)
        self.gfin = gfin
        self.yf = [P.sb(f"yf{k}", [128, 512], F32, at=arena + k * 2048) for k in range(8)]
        self.ost = [P.sb(f"ost{k}", [128, D], F32, at=arena + 34816 + 6144 + k * 4096) for k in range(2)]
        self.ffn(2, wf_in[1], wf_out[1], tail=lambda tb: self.norm_block(3, tb))
        self.final_out(gfin)
        P.emit()
        return P

    def finish_debug(self, dbg_x, dbg_h, dbg_m):
        xT, hT = self.xT, self.hT
        self.dma("sp", dbg_x, xT[:, :, :], "dbg0", reads=[xT[:, :, :]], dwrites=["dbgx"])
        self.dma("sp", dbg_h, hT[:, :, :], "dbg1", reads=[hT[:, :, :]], dwrites=["dbgh"])
        self.dma("sp", dbg_m[:, 0:72], self.modT[:, :], "dbg2", reads=[self.modT[:, :]], dwrites=["dbgm"])
        keys = ["dbgx", "dbgh", "dbgm"]
        if self.stage in ("ret", "diff"):
            keys += [("ret_sp", hh, tb) for hh in range(4) for tb in range(8)]
        if self.stage == "diff":
            keys += [("diff_sp", hh) for hh in range(8)]
        self.P.op("sp", lambda e: e.nop(), dreads=keys)
        self.P.emit()
        return self.P

    def mod_groups(self, wcond_d, bcond, gnorm, i):
        ps, modT, gs, gatef = self.ps, self.modT, self.gs, self.gatef
        bank = ps[7]
        for j in range(3):
            g = i * 3 + j
            (w,) = self.wload("A", [(wcond_d[:, g * D:(g + 1) * D], KC, D)],
                              after=[("xk", 9)] if (i == 0 and j == 0) else ())
            for fc in range(KC):
                col = g * 8 + fc
                for kc in range(KC):
                    self.mm(bank[:, col:col + 1], w[:, kc, fc * 128:(fc + 1) * 128], self.c_bf_ap(kc),
                            kc == 0, kc == KC - 1)
        c0 = i * 24
        self.tt("dve", modT[:, c0:c0 + 24], bank[:, c0:c0 + 24], bcond[:, c0:c0 + 24], ALU.add)
        self.stt("dve", gs[:, i * 8:i * 8 + 8], modT[:, c0 + 8:c0 + 16], 1.0, gnorm[:, i * 8:i * 8 + 8],
                 ALU.add, ALU.mult)
        self.ts("dve", gatef[:, i * 8:i * 8 + 8], modT[:, c0 + 16:c0 + 24], 1.0 if i == 1 else 0.5, ALU.mult)

    def mod_half(self, wcond_d, g, half):
        P, ps = self.P, self.ps
        slot = self.wC
        v = slot[:, :].rearrange("p (k c) -> p k c", k=KC)
        src = wcond_d[:, g * D + half * 512:g * D + (half + 1) * 512]
        self.dma("pool", v, src.rearrange("(k p) c -> p k c", p=128), "wC", writes=[v])
        bank = ps[7]
        for f4 in range(4):
            col = g * 8 + half * 4 + f4
            for kc in range(KC):
                self.mm(bank[:, col:col + 1], v[:, kc, f4 * 128:(f4 + 1) * 128], self.c_bf_ap(kc),
                        kc == 0, kc == KC - 1)

    def mod_finish(self, bcond, gnorm, i):
        ps, modT, gs, gatef = self.ps, self.modT, self.gs, self.gatef
        bank = ps[7]
        c0 = i * 24
        self.tt("dve", modT[:, c0:c0 + 24], bank[:, c0:c0 + 24], bcond[:, c0:c0 + 24], ALU.add)
        self.stt("dve", gs[:, i * 8:i * 8 + 8], modT[:, c0 + 8:c0 + 16], 1.0, gnorm[:, i * 8:i * 8 + 8],
                 ALU.add, ALU.mult)
        self.ts("dve", gatef[:, i * 8:i * 8 + 8], modT[:, c0 + 16:c0 + 24], 1.0 if i == 1 else 0.5, ALU.mult)

    def c_bf_ap(self, kc):
        return self._c_bf[:, kc:kc + 1]

    def norm_modulate(self, i):
        self.norm_stats(i)
        self.norm_apply(i)

    def norm_bufs(self, i):
        P, A = self.P, self.arena
        sq = [P.sb(f"sq{i}_{k}", [128, 512], F32, at=A + k * 2048) for k in range(2)]
        vt = [P.sb(f"vt{i}_{k}", [128, 512], F32, at=A + 4096 + k * 2048) for k in range(2)]
        rstd = [P.sb(f"rstd{i}_{k}", [128, 512], F32, at=A + 8192 + k * 2048) for k in range(4)]
        return sq, vt, rstd

    def norm_stats(self, i):
        ps, xT = self.ps, self.xT
        sq, vt, rstd = self.norm_bufs(i)
        self._rstd = rstd
        for tb in range(NB):
            tsl = slice(tb * 512, (tb + 1) * 512)
            bank = ps[4 + self.rot("psn", 2)]
            for dc in range(KC):
                s_ = sq[self.rot("sq", 2)]
                self.act(s_[:, :], xT[:, dc, tsl], AF.Square)
                self.mm(bank[:, :], self.ones32[:, :], s_[:, :], dc == 0, dc == KC - 1)
            v = vt[tb % 2]
            self.ts("dve", v[:, :], bank[:, :], 1.0 / D, ALU.mult, EPS, ALU.add)
            self.act(v[:, :], v[:, :], AF.Sqrt)
            r = rstd[tb]
            self.P.op("dve", lambda e, r=r, v=v: e.reciprocal(out=r[:, :], in_=v[:, :]),
                      reads=[v[:, :]], writes=[r[:, :]])

    def norm_block(self, i, tb):
        P, ps, xT, hT = self.P, self.ps, self.xT, self.hT
        B0 = self.arena + 34816
        sq = [P.sb(f"nbsq{i}_{tb}_{k}", [128, 512], F32, at=B0 + k * 2048) for k in range(2)]
        if i == 3:
            v = P.sb(f"nbr{i}_{tb}", [128, 512], F32, at=B0 + 4096)
        else:
            v = P.sb(f"nbv{i}_{tb}", [128, 512], F32, at=B0 + 4096)
            tmp = [P.sb(f"nbt{i}_{tb}_{k}", [128, 512], F32, at=B0 + 6144 + k * 2048) for k in range(2)]
        tsl = slice(tb * 512, (tb + 1) * 512)
        bank = ps[6 + self.rot("psnb", 2)]
        for dc in range(KC):
            s_ = sq[self.rot("sq", 2)]
            self.act(s_[:, :], xT[:, dc, tsl], AF.Square)
            self.mm(bank[:, :], self.ones32[:, :], s_[:, :], dc == 0, dc == KC - 1)
        self.ts("dve", v[:, :], bank[:, :], 1.0 / D, ALU.mult, EPS, ALU.add)
        self.act(v[:, :], v[:, :], AF.Sqrt)
        self.P.op("dve", lambda e: e.reciprocal(out=v[:, :], in_=v[:, :]), reads=[v[:, :]], writes=[v[:, :]])
        if i == 3:
            self.final_block(tb, v)
            return
        for dc in range(KC):
            t = tmp[self.rot("ntmp", 2)]
            self.stt("dve", t[:, :], xT[:, dc, tsl], self.gs[:, i * 8 + dc:i * 8 + dc + 1], v[:, :],
                     ALU.mult, ALU.mult)
            self.act(hT[:, dc, tsl], t[:, :], AF.Identity, bias=self.modT[:, i * 24 + dc:i * 24 + dc + 1])

    def norm_apply(self, i):
        P, A, xT, hT = self.P, self.arena, self.xT, self.hT
        rstd = self._rstd
        tmp = [P.sb(f"ntmp{i}_{k}", [128, 512], F32, at=A + 16384 + k * 2048) for k in range(2)]
        for tb in range(NB):
            tsl = slice(tb * 512, (tb + 1) * 512)
            r = rstd[tb]
            if i == 3:
                self.final_block(tb, r)
                continue
            for dc in range(KC):
                t = tmp[self.rot("ntmp", 2)]
                self.stt("dve", t[:, :], xT[:, dc, tsl], self.gs[:, i * 8 + dc:i * 8 + dc + 1], r[:, :],
                         ALU.mult, ALU.mult)
                self.act(hT[:, dc, tsl], t[:, :], AF.Identity, bias=self.modT[:, i * 24 + dc:i * 24 + dc + 1])

    def ffn(self, i, w_in_d, w_out_d, extra=None, tail=None):
        P, ps, xT, hT = self.P, self.ps, self.xT, self.hT
        A = self.arena
        actb = [P.sb(f"actb{i}_{k}", [128, 4, S], BF16, at=A + k * 16384) for k in range(2)]
        sa = [P.sb(f"sa{i}_{k}", [128, 512], BF16, at=A + 32768 + k * 1024) for k in range(2)]
        groups = [(g * 512, 512) for g in range(5)] + [(2560, 256)]
        for gi, (f0, fw) in enumerate(groups):
            nf = fw // 128
            wa, wb = self.wload("A", [(w_in_d[:, f0:f0 + fw], KC, fw),
                                      (w_in_d[:, DFF + f0:DFF + f0 + fw], KC, fw)])
            (wo,) = self.wload("B", [(w_out_d[f0:f0 + fw, :], nf, D)])
            ab = actb[gi % 2]
            for tb in range(NB):
                tsl = slice(tb * 512, (tb + 1) * 512)
                for ft in range(nf):
                    pa = ps[self.rot("ffa", 2)]
                    pb = ps[2 + self.rot("ffb", 2)]
                    for kc in range(KC):
                        self.mm(pa[:, :], wa[:, kc, ft * 128:(ft + 1) * 128], hT[:, kc, tsl], kc == 0, kc == KC - 1)
                    for kc in range(KC):
                        self.mm(pb[:, :], wb[:, kc, ft * 128:(ft + 1) * 128], hT[:, kc, tsl], kc == 0, kc == KC - 1)
                    s = sa[self.rot("sa", 2)]
                    self.act(s[:, :], pa[:, :], AF.Silu)
                    self.tt("dve", ab[:, ft, tsl], pb[:, :], s[:, :], ALU.mult)
            if extra is not None:
                extra(gi)
            for tb in range(NB):
                tsl = slice(tb * 512, (tb + 1) * 512)
                for dc in range(KC):
                    po = ps[4 + self.rot("ffo", 2)]
                    for ft in range(nf):
                        self.mm(po[:, :], wo[:, ft, dc * 128:(dc + 1) * 128], ab[:, ft, tsl], ft == 0, ft == nf - 1)
                    self.stt("dve", xT[:, dc, tsl], po[:, :], self.gatef[:, i * 8 + dc:i * 8 + dc + 1],
                             xT[:, dc, tsl], ALU.mult, ALU.add)
                if tail is not None and gi == len(groups) - 1 and tb > 0:
                    tail(tb - 1)
        if tail is not None:
            tail(NB - 1)

    def retention(self, win_d, kdec, xi, m01, ret_sp):
        P, ps, hT = self.P, self.ps, self.hT
        A = self.arena
        o = [0]

        def al(name, shape, dtype):
            n = int(np.prod(shape[1:])) * ISZ[dtype]
            t = P.sb(name, shape, dtype, at=A + o[0])
            o[0] += (n + 255) // 256 * 256
            assert o[0] <= self.asz, ("ret arena", o[0], self.asz)
            return t
        qT = [al(f"r_qT{k}", [128, 2, 512], BF16) for k in range(2)]
        kT = [al(f"r_kT{k}", [128, 2, 512], BF16) for k in range(2)]
        vb = [al(f"r_v{k}", [128, 4, 512], BF16) for k in range(2)]
        sg = [al(f"r_sg{k}", [128, 4, 512], BF16) for k in range(2)]
        R32 = al("r_R32", [128, 2, 512], F32)
        Rbf = al("r_Rbf", [128, 2, 512], BF16)
        k2 = [al(f"r_k2{k}", [128, 256], BF16) for k in range(2)]
        PT = [al(f"r_PT{k}", [128, 128], BF16) for k in range(2)]
        osb = [al(f"r_osb{k}", [128, 512], F32) for k in range(2)]
        nbf = [al(f"r_nbf{k}", [128, 512], BF16) for k in range(2)]
        gtd = [al(f"r_gtd{k}", [128, 512], BF16) for k in range(2)]
        gst = [al(f"r_gst{k}", [128, 4, 512], BF16) for k in range(2)]
        st6 = [al(f"r_st{k}", [128, 6], F32) for k in range(2)]
        mv = [al(f"r_mv{k}", [128, 8], F32) for k in range(2)]
        psT6 = ps[6][:, :].bitcast(BF16)
        psT7 = ps[7][:, :].bitcast(BF16)
        W = {}

        def load_head(h):
            wq, wk, wv = self.wload("A", [(win_d[:, O_RQ + h * 256:O_RQ + (h + 1) * 256], KC, 256),
                                          (win_d[:, O_RK + h * 256:O_RK + (h + 1) * 256], KC, 256),
                                          (win_d[:, O_RV + h * 512:O_RV + (h + 1) * 512], KC, 512)])
            (wg,) = self.wload("B", [(win_d[:, O_RG + h * 512:O_RG + (h + 1) * 512], KC, 512)])
            W[h] = (wq, wk, wv, wg)

        def proj_piece(bidx, c):
            h, tb = divmod(bidx, 4)
            wq, wk, wv, wg = W[h]
            bi = bidx % 2
            tsl = slice(tb * 512, (tb + 1) * 512)
            bank = ps[self.rot("rpj", 2)]
            if c < 2:
                for kc in range(KC):
                    self.mm(bank[:, :], wq[:, kc, c * 128:(c + 1) * 128], hT[:, kc, tsl], kc == 0, kc == KC - 1)
                self.cp("act", qT[bi][:, c, :], bank[:, :])
            else:
                dcq = c - 2
                for kc in range(KC):
                    self.mm(bank[:, :], wk[:, kc, dcq * 128:(dcq + 1) * 128], hT[:, kc, tsl], kc == 0, kc == KC - 1)
                self.tt("dve", kT[bi][:, dcq, :].rearrange("p (a b) -> p a b", a=4),
                        bank[:, :].rearrange("p (a b) -> p a b", a=4),
                        kdec[:, h:h + 1, :].to_broadcast([128, 4, 128]), ALU.mult)
            tok = slice(tb * 512 + c * 128, tb * 512 + (c + 1) * 128)
            bank = ps[self.rot("rpj", 2)]
            for kc in range(KC):
                self.mm(bank[:, :], hT[:, kc, tok], wv[:, kc, :], kc == 0, kc == KC - 1)
            self.cp("act", vb[bi][:, c, :], bank[:, :])
            bank = ps[self.rot("rpj", 2)]
            for kc in range(KC):
                self.mm(bank[:, :], hT[:, kc, tok], wg[:, kc, :], kc == 0, kc == KC - 1)
            self.act(sg[bi][:, c, :], bank[:, :], AF.Silu)

        def tg(bidx, c):
            h, tb = divmod(bidx, 4)
            bi = bidx % 2
            j = (bidx * 4 + c) % 2
            cs = slice(c * 128, (c + 1) * 128)
            for e4 in range(4):
                self.tr(psT7[:, e4 * 128:(e4 + 1) * 128], gtd[j][:, e4 * 128:(e4 + 1) * 128], self.idb[:, :])
            self.cp("act", gst[bi][:, :, cs], psT7[:, 0:512].rearrange("p (a b) -> p a b", a=4))
            if c == 3:
                for half in range(2):
                    src = gst[bi][:, :, half * 256:(half + 1) * 256]
                    self.dma("sp", ret_sp[2 * tb + half, :, h * 4:(h + 1) * 4, :], src, f"rsp{bi}{half}",
                             reads=[src], dwrites=[("ret_sp", h, 2 * tb + half)])

        def e2(bidx, c):
            bi = bidx % 2
            j = (bidx * 4 + c) % 2
            m = mv[j]
            self.act(nbf[j][:, :], osb[j][:, :], AF.Identity, scale=m[:, 3:4], bias=m[:, 4:5])
            self.tt("dve", gtd[j][:, :], nbf[j][:, :], sg[bi][:, c, :], ALU.mult)

        load_head(0)
        for c in range(4):
            proj_piece(0, c)
        pend1 = None
        pend2 = None
        for bidx in range(16):
            h, tb = divmod(bidx, 4)
            bi = bidx % 2
            gam = 1.0 - 2.0 ** (-5 - h)
            gC = float(gam ** 128)
            if tb == 0:
                if h + 1 < 4:
                    load_head(h + 1)
                self.memset("pool", R32[:, :, :], 0.0)
            for c in range(4):
                n = tb * 4 + c
                cs = slice(c * 128, (c + 1) * 128)
                j = (bidx * 4 + c) % 2
                last = (n == NT - 1)
                for dcq in range(2):
                    self.mm(ps[2][:, 0:128], kT[bi][:, dcq, cs], qT[bi][:, dcq, cs], dcq == 0, dcq == 1)
                self.tt("dve", PT[j][:, :], ps[2][:, 0:128], m01[:, :], ALU.mult)
                if not last:
                    for dcq in range(2):
                        self.tr(psT6[:, dcq * 128:(dcq + 1) * 128], kT[bi][:, dcq, cs], self.idb[:, :])
                    self.act(k2[j][:, :], psT6[:, 0:256], AF.Copy, scale=gC)
                if bidx + 1 < 16:
                    proj_piece(bidx + 1, c)
                new_e2 = None
                if pend1 is not None:
                    e2(*pend1)
                    new_e2 = pend1
                    pend1 = None
                self.mm(ps[3][:, :], PT[j][:, :], vb[bi][:, c, :], True, n == 0)
                if n > 0:
                    for dcq in range(2):
                        self.mm(ps[3][:, :], qT[bi][:, dcq, cs], Rbf[:, dcq, :], False, dcq == 1)
                if not last:
                    for dcq in range(2):
                        self.mm(ps[4 + dcq][:, :], k2[j][:, dcq * 128:(dcq + 1) * 128], vb[bi][:, c, :], True, True)
                    for dcq in range(2):
                        self.stt("dve", R32[:, dcq, :], R32[:, dcq, :], gC, ps[4 + dcq][:, :], ALU.mult, ALU.add)
                        self.cp("dve", Rbf[:, dcq, :], R32[:, dcq, :])
                if pend2 is not None:
                    tg(*pend2)
                    pend2 = None
                self.act(osb[j][:, :], ps[3][:, :], AF.Identity, scale=xi[:, h:h + 1])
                s6, m = st6[j], mv[j]
                P.op("dve", lambda e, s6=s6, ob=osb[j]: e.bn_stats(out=s6[:, :], in_=ob[:, :]),
                     reads=[osb[j][:, :]], writes=[s6[:, :]])
                P.op("dve", lambda e, s6=s6, m=m: e.bn_aggr(out=m[:, 0:2], in_=s6[:, :]),
                     reads=[s6[:, :]], writes=[m[:, 0:2]])
                self.ts("dve", m[:, 2:3], m[:, 1:2], EPS, ALU.add)
                self.tt("pool", m[:, 3:4], m[:, 2:3], self.nhalf[:, 0:1], ALU.pow)
                self.stt("dve", m[:, 4:5], m[:, 0:1], -1.0, m[:, 3:4], ALU.mult, ALU.mult)
                pend1 = (bidx, c)
                pend2 = new_e2
        e2(*pend1)
        tg(*pend2)
        tg(*pend1)

    def diffattn(self, win_d, qx_d, kx_d, umask, sublnB, neglam, diff_sp):
        P, ps, hT = self.P, self.ps, self.hT
        A = self.arena
        o = [0]

        def al(name, shape, dtype):
            n = int(np.prod(shape[1:])) * ISZ[dtype]
            t = P.sb(name, shape, dtype, at=A + o[0])
            o[0] += (n + 255) // 256 * 256
            assert o[0] <= self.asz, ("diff arena", o[0], self.asz)
            return t
        Qs = [[al(f"d_Q{s_}{m}", [128, S], BF16) for m in range(2)] for s_ in range(2)]
        Ks = [[al(f"d_K{s_}{m}", [128, S], BF16) for m in range(2)] for s_ in range(2)]
        Vs = [al(f"d_V{s_}", [128, NT, 132], BF16) for s_ in range(2)]
        NP_ = 3
        Psb = [al(f"d_P{k}", [128, 512], BF16) for k in range(NP_)]
        rrb = [al(f"d_rr{k}", [128, 16], F32) for k in range(2)]
        tmp1 = al("d_tmp", [128, 2, 2, 128], F32)
        osb1 = al("d_o", [128, 2, 128], F32)
        sq1 = al("d_sq", [128, 2, 128], F32)
        t21 = al("d_t2", [128, 2, 128], F32)
        tmpb, osbb, sqb, t2b = [tmp1, tmp1], [osb1, osb1], [sq1, sq1], [t21, t21]
        nrmb = [al(f"d_n{k}", [128, 2, 128], BF16) for k in range(2)]
        dst1 = al("d_dst", [128, S], BF16)
        dst = [dst1, dst1]
        lamvec = al("d_lamvec", [128, 2], F32)
        psT7 = ps[7][:, :].bitcast(BF16)
        psbig = self.psbig
        for s_ in range(2):
            self.memset("pool", Qs[s_][1][0:64, :], 0.0)
            self.memset("pool", Ks[s_][1][0:64, :], 0.0)
            self.memset("pool", Vs[s_][:, :, 128:129], 1.0)
        self.memset("dve", lamvec[:, 0:1], 1.0)
        self.cp("dve", lamvec[:, 1:2], neglam[:, :])
        krows = [(0, 70), (0, 128)]
        LAG = 2
        NPAIR = NT // 2

        def sbank():
            return ps[(0, 1, 6)[self.rot("dsc", 3)]]

        def load_w(hh):
            return self.wload("B", [(win_d[:, O_DQ + hh * 128:O_DQ + (hh + 1) * 128], KC, 128),
                                    (win_d[:, O_DK + hh * 128:O_DK + (hh + 1) * 128], KC, 128),
                                    (win_d[:, O_DV + hh * 128:O_DV + (hh + 1) * 128], KC, 128)])
        def extras(hh):
            Q, Kt = Qs[hh % 2], Ks[hh % 2]
            sx = hh % 2
            self.dma("sp", Q[0][64:70, :], qx_d[hh], f"dx{sx}0", writes=[Q[0][64:70, :]])
            self.dma("sp", Kt[0][64:70, :], kx_d[hh], f"dx{sx}1", writes=[Kt[0][64:70, :]])
            self.dma("sp", Q[1][0:6, :], qx_d[hh], f"dx{sx}2", writes=[Q[1][0:6, :]])
            self.dma("sp", Kt[1][0:6, :], kx_d[hh], f"dx{sx}3", writes=[Kt[1][0:6, :]])

        def make_pieces(hh, W_):
            wq, wk, wv = W_
            Q, Kt, V = Qs[hh % 2], Ks[hh % 2], Vs[hh % 2]
            pieces = []

            def pq(tb):
                tsl = slice(tb * 512, (tb + 1) * 512)
                bank = sbank()
                for kc in range(KC):
                    self.mm(bank[:, :], wq[:, kc, :], hT[:, kc, tsl], kc == 0, kc == KC - 1)
                self.ts("dve", Q[0][0:64, tsl], bank[0:64, :], 0.125, ALU.mult)
                self.ts("dve", Q[1][64:128, tsl], bank[64:128, :], 0.125, ALU.mult)

            def pk(tb):
                tsl = slice(tb * 512, (tb + 1) * 512)
                bank = sbank()
                for kc in range(KC):
                    self.mm(bank[:, :], wk[:, kc, :], hT[:, kc, tsl], kc == 0, kc == KC - 1)
                self.cp("dve", Kt[0][0:64, tsl], bank[0:64, :])
                self.cp("dve", Kt[1][64:128, tsl], bank[64:128, :])

            def pvv(t4):
                bank = sbank()
                for j in range(4):
                    t_ = t4 * 4 + j
                    tok = slice(t_ * 128, (t_ + 1) * 128)
                    for kc in range(KC):
                        self.mm(bank[:, j * 128:(j + 1) * 128], hT[:, kc, tok], wv[:, kc, :], kc == 0, kc == KC - 1)
                self.cp("dve", V[:, t4 * 4:t4 * 4 + 4, 0:128], bank[:, :].rearrange("p (a b) -> p a b", a=4))
            for i_ in range(4):
                pieces.append(lambda i_=i_: pq(i_))
                pieces.append(lambda i_=i_: pk(i_))
                pieces.append(lambda i_=i_: pvv(i_))
            return pieces

        Wcur = load_w(0)
        extras(0)
        for f in make_pieces(0, Wcur):
            f()
        for h in range(8):
            Q, Kt, V = Qs[h % 2], Ks[h % 2], Vs[h % 2]
            pieces = []
            if h + 1 < 8:
                Wnext = load_w(h + 1)
                extras(h + 1)
                pieces = make_pieces(h + 1, Wnext)
            nunit = [0]

            def qk_exp(p, ka, m):
                r0, r1 = krows[m]
                qa = 2 * p
                q2 = slice(qa * 128, (qa + 2) * 128)
                sb_ = sbank()
                if ka < qa:
                    for kl in range(2):
                        kt = ka + kl
                        self.mm(sb_[:, kl * 256:(kl + 1) * 256], Kt[m][r0:r1, kt * 128:(kt + 1) * 128],
                                Q[m][r0:r1, q2], True, True)
                    ncol = 512
                    items = [(ka, 0, 0), (ka, 1, 128), (ka + 1, 0, 256), (ka + 1, 1, 384)]
                else:
                    self.mm(sb_[:, 0:256], Kt[m][r0:r1, qa * 128:(qa + 1) * 128], Q[m][r0:r1, q2], True, True,
                            skip=True)
                    self.mm(sb_[:, 0:128], self.idb[:, :], umask[:, :], False, True, skip=True)
                    qb_ = slice((qa + 1) * 128, (qa + 2) * 128)
                    self.mm(sb_[:, 256:384], Kt[m][r0:r1, qb_], Q[m][r0:r1, qb_], True, True, skip=True)
                    self.mm(sb_[:, 256:384], self.idb[:, :], umask[:, :], False, True, skip=True)
                    ncol = 384
                    items = [(qa, 0, 0), (qa, 1, 128), (qa + 1, 1, 256)]
                pb = Psb[self.rot("dP", NP_)]
                self.act(pb[:, 0:ncol], sb_[:, 0:ncol], AF.Exp)
                return (p, ka, m, items, pb)

            def pv(u):
                p, ka, m, items, pb = u
                for (kt, j, c0) in items:
                    qt = 2 * p + j
                    po = ps[2 + (qt % 4)]
                    self.mm(po[:, m * 256:m * 256 + 129], pb[:, c0:c0 + 128], V[:, kt, 0:129],
                            (kt == 0 and m == 0), kt == qt, skip=True)
                if m == 1 and ka == 2 * p:
                    epilogue(p)
                    if p > 0:
                        epi2(p - 1)

            def epi2(p):
                k2_ = p % 2
                for j in range(2):
                    self.tr(psT7[:, j * 128:(j + 1) * 128], nrmb[k2_][:, j, :], self.idb[:, :])
                self.cp("act", dst[h % 2][:, 2 * p * 128:(2 * p + 2) * 128], psT7[:, 0:256])

            def epilogue(p):
                k2_ = p % 2
                b0 = (2 * p) % 4
                pp = psbig[:, b0 * 512:(b0 + 2) * 512]
                pp4 = pp.rearrange("p (b m c) -> p b m c", b=2, m=2)
                r = rrb[k2_]
                r4 = r[:, 0:4].rearrange("p (b m) -> p b m", b=2)
                rl4 = r[:, 4:8].rearrange("p (b m) -> p b m", b=2)
                tmp, ob, sq, t2, nb_ = tmpb[k2_], osbb[k2_], sqb[k2_], t2b[k2_], nrmb[k2_]
                P.op("dve", lambda e: e.reciprocal(out=r4, in_=pp4[:, :, :, 128]), reads=[pp], writes=[r[:, 0:4]])
                self.tt("dve", rl4, r4, lamvec[:, :].unsqueeze(1).to_broadcast([128, 2, 2]), ALU.mult)
                P.op("dve", lambda e: e.tensor_tensor(out=tmp[:, :, :, :], in0=pp4[:, :, :, 0:128],
                                                      in1=rl4.unsqueeze(3).to_broadcast([128, 2, 2, 128]),
                                                      op=ALU.mult),
                     reads=[pp, r[:, 4:8]], writes=[tmp[:, :, :, :]])
                self.tt("dve", ob[:, :, :], tmp[:, :, 0, :], tmp[:, :, 1, :], ALU.add)
                self.tt("dve", sq[:, :, :], ob[:, :, :], ob[:, :, :], ALU.mult)
                P.op("dve", lambda e: e.reduce_sum(out=r[:, 8:10], in_=sq[:, :, :], axis=mybir.AxisListType.X),
                     reads=[sq[:, :, :]], writes=[r[:, 8:10]])
                self.ts("dve", r[:, 10:12], r[:, 8:10], 1.0 / 128, ALU.mult, EPS, ALU.add)
                self.tt("pool", r[:, 12:14], r[:, 10:12], self.nhalf[:, 0:2], ALU.pow)
                self.tt("dve", t2[:, :, :], ob[:, :, :], r[:, 12:14].unsqueeze(2).to_broadcast([128, 2, 128]),
                        ALU.mult)
                self.tt("dve", nb_[:, :, :], t2[:, :, :], sublnB[:, :].unsqueeze(1).to_broadcast([128, 2, 128]),
                        ALU.mult)

            pending = []
            for p in range(NPAIR):
                for ka in range(0, 2 * p + 2, 2):
                    for m in range(2):
                        pending.append(qk_exp(p, ka, m))
                        if len(pending) > LAG:
                            pv(pending.pop(0))
                        nunit[0] += 1
                        if nunit[0] % 6 == 0 and pieces:
                            pieces.pop(0)()
            while pending:
                pv(pending.pop(0))
            while pieces:
                pieces.pop(0)()
            epi2(NPAIR - 1)
            self.dma("sp", diff_sp[:, :, h, :].rearrange("b p t -> p b t"),
                     dst[h % 2][:, :].rearrange("p (b t) -> p b t", b=8), f"dsp{h % 2}",
                     reads=[dst[h % 2][:, :]], dwrites=[("diff_sp", h)])

    def merge(self, win_d, wro_d, wdo_d, wout_d, ret_sp, diff_sp, tail=None):
        P, ps, hT, xT = self.P, self.ps, self.hT, self.xT
        A = self.arena
        o = [0]

        def al(name, shape, dtype):
            n = int(np.prod(shape[1:])) * ISZ[dtype]
            t = P.sb(name, shape, dtype, at=A + o[0])
            o[0] += (n + 255) // 256 * 256
            assert o[0] <= self.asz, ("merge arena", o[0], self.asz)
            return t
        TB = 256
        gin = [al(f"m_gin{k}", [128, 16, TB], BF16) for k in range(2)]
        din_ = [al(f"m_din{k}", [128, 8, TB], BF16) for k in range(2)]
        s0 = [al(f"m_s0{k}", [128, TB], F32) for k in range(2)]
        s1 = [al(f"m_s1{k}", [128, TB], F32) for k in range(2)]
        tq = [al(f"m_t{k}", [128, TB], F32) for k in range(2)]
        uq = [al(f"m_u{k}", [128, TB], F32) for k in range(2)]
        yT = [al(f"m_y{k}", [128, 2, TB], BF16) for k in range(2)]
        def load_q(qd):
            c0 = qd * 256
            a = self.wload("A", [(wro_d[:, c0:c0 + 256], 16, 256),
                                 (wdo_d[:, c0:c0 + 256], 8, 256),
                                 (win_d[:, O_G0 + c0:O_G0 + c0 + 256], 8, 256)])
            b = self.wload("B", [(win_d[:, O_G1 + c0:O_G1 + c0 + 256], 8, 256),
                                 (wout_d[c0:c0 + 256, :], 2, D)])
            return a + b

        def outproj(wou, tb, lastq=False):
            tsl = slice(tb * TB, (tb + 1) * TB)
            bi = tb % 2
            for dc in range(KC):
                po = ps[self.rot("mps", 8)]
                for yc in range(2):
                    self.mm(po[:, 0:TB], wou[:, yc, dc * 128:(dc + 1) * 128], yT[bi][:, yc, :], yc == 0, yc == 1)
                self.stt("dve", xT[:, dc, tsl], po[:, 0:TB], self.gatef[:, 8 + dc:8 + dc + 1], xT[:, dc, tsl],
                         ALU.mult, ALU.add)
            if lastq and tail is not None and tb % 2 == 1:
                tail(tb // 2)

        Wn = load_q(0)
        pend = None
        for qd in range(4):
            wro, wdo, wg0, wg1, wou = Wn
            if pend is not None:
                outproj(*pend)
                pend = None
            if qd + 1 < 4:
                Wn = load_q(qd + 1)
            for tb in range(S // TB):
                tsl = slice(tb * TB, (tb + 1) * TB)
                bi = tb % 2
                self.dma("sp", gin[bi][:, :, :], ret_sp[tb], f"mgi{bi}",
                         writes=[gin[bi][:, :, :]], dreads=[("ret_sp", hh, tb) for hh in range(4)])
                self.dma("sp", din_[bi][:, :, :], diff_sp[tb], f"mdi{bi}",
                         writes=[din_[bi][:, :, :]], dreads=[("diff_sp", hh) for hh in range(8)])
                for yc in range(2):
                    ys = slice(yc * 128, (yc + 1) * 128)
                    k = self.rot("mrot", 2)
                    pg0 = ps[self.rot("mps", 8)]
                    for kc in range(8):
                        self.mm(pg0[:, 0:TB], wg0[:, kc, ys], hT[:, kc, tsl], kc == 0, kc == 7)
                    pg1 = ps[self.rot("mps", 8)]
                    for kc in range(8):
                        self.mm(pg1[:, 0:TB], wg1[:, kc, ys], hT[:, kc, tsl], kc == 0, kc == 7)
                    self.act(s0[k][:, :], pg0[:, 0:TB], AF.Sigmoid)
                    self.act(s1[k][:, :], pg1[:, 0:TB], AF.Sigmoid)
                    pr = ps[self.rot("mps", 8)]
                    for kc in range(16):
                        self.mm(pr[:, 0:TB], wro[:, kc, ys], gin[bi][:, kc, :], kc == 0, kc == 15)
                    pd = ps[self.rot("mps", 8)]
                    for kc in range(8):
                        self.mm(pd[:, 0:TB], wdo[:, kc, ys], din_[bi][:, kc, :], kc == 0, kc == 7)
                    self.tt("dve", tq[k][:, :], pr[:, 0:TB], s0[k][:, :], ALU.mult)
                    self.tt("dve", uq[k][:, :], pd[:, 0:TB], s1[k][:, :], ALU.mult)
                    self.tt("dve", yT[bi][:, yc, :], tq[k][:, :], uq[k][:, :], ALU.add)
                if pend is not None:
                    outproj(*pend)
                pend = (wou, tb, qd == 3)
        outproj(*pend)

    def final_out(self, gfin):
        self.P.op("sp", lambda e: e.nop(), dreads=[f"out{t}" for t in range(NT)])

    def final_block(self, tb, r):
        ps, xT = self.ps, self.xT
        tsl = slice(tb * 512, (tb + 1) * 512)
        for dc in range(KC):
            self.stt("dve", self.yf[dc][:, :], xT[:, dc, tsl], self.gfin[:, dc:dc + 1], r[:, :], ALU.mult, ALU.mult)
        for c in range(4):
            t_ = tb * 4 + c
            ost = self.ost[t_ % 2]
            for half in range(2):
                bank = ps[self.rot("pso", 4)]
                for j in range(4):
                    dc = half * 4 + j
                    self.tr(bank[:, j * 128:(j + 1) * 128], self.yf[dc][:, c * 128:(c + 1) * 128], self.id32[:, :])
                self.cp("act" if half == 0 else "dve", ost[:, half * 512:(half + 1) * 512], bank[:, :])
            self.dma("sp", self.out_d[t_ * 128:(t_ + 1) * 128, :], ost[:, :], f"xout{t_ % 2}", reads=[ost[:, :]],
                     dwrites=[f"out{t_}"])


def _bf(a):
    return a.astype(ml_dtypes.bfloat16).astype(np.float64)


def host_consts():
    c = {}
    c["c_ident"] = np.eye(128, dtype=np.float32)
    pos = np.arange(128, dtype=np.float64)
    kdec = np.zeros((4, 128, 128), np.float32)
    xi = np.zeros((128, 4), np.float32)
    for h in range(4):
        log_g = np.log1p(-np.exp2(-5.0 - h))
        row = np.exp(-(pos + 1.0) * log_g) / 16.0
        kdec[h] = row[None, :].astype(np.float32)
        xi[:, h] = np.exp((pos + 1.0) * log_g).astype(np.float32)
    c["c_kdec"] = kdec
    c["c_xi"] = xi
    r = np.arange(128)
    c["c_mask01"] = (r[None, :] >= r[:, None]).astype(np.float32)
    c["c_umask"] = np.where(r[None, :] < r[:, None], NEG, 0.0).astype(np.float32)
    p = np.arange(S, dtype=np.float64)
    qx = np.zeros((8, 6, S), np.float32)
    kx = np.zeros((8, 6, S), np.float32)
    for h in range(8):
        slope = 2.0 ** (-8.0 * (h + 1.0) / 8.0)
        a = slope * p
        a1 = _bf(a)
        a2 = _bf(a - a1)
        a3 = _bf(a - a1 - a2)
        qx[h, 0], qx[h, 1], qx[h, 2] = -a1, -a2, -a3
        qx[h, 3:6] = 1.0
        kx[h, 0:3] = 1.0
        kx[h, 3], kx[h, 4], kx[h, 5] = a1, a2, a3
    c["c_qx"] = qx.astype(ml_dtypes.bfloat16)
    c["c_kx"] = kx.astype(ml_dtypes.bfloat16)
    return c


def pvec(v, n):
    return np.ascontiguousarray(np.asarray(v, np.float32).reshape(n, 128).T)


def make_in_maps(inputs):
    f = lambda k: np.ascontiguousarray(np.asarray(inputs[k], np.float32))
    shared = dict(
        w_cond=f("w_cond")[0], b_cond=pvec(f("b_cond")[0], 72),
        g_norm=np.ascontiguousarray(np.concatenate([pvec(f("g_norm")[0, i], 8) for i in range(3)], axis=1)),
        w_ffn1_in=f("w_ffn1_in")[0], w_ffn1_out=f("w_ffn1_out")[0], w_in=f("w_in")[0],
        w_ret_out=f("w_ret_out")[0], diff_lambda=f("diff_lambda")[0].reshape(256),
        diff_subln=f("diff_subln")[0].reshape(128), w_diff_out=f("w_diff_out")[0], w_out=f("w_out")[0],
        w_ffn2_in=f("w_ffn2_in")[0], w_ffn2_out=f("w_ffn2_out")[0], g_final=pvec(f("g_final"), 8),
    )
    shared.update(host_consts())
    x = f("x")
    c = f("c")
    maps = []
    for b in range(8):
        m = dict(shared)
        m["x"] = np.ascontiguousarray(x[b])
        m["c"] = pvec(c[b], 8)
        maps.append(m)
    return maps


def build_nc(stage=None):
    nc = bass.Bass("TRN2", target_bir_lowering=False)
    k = K(nc, stage)
    P = k.build()
    return nc, P


def kernel(**inputs):
    nc, P = build_nc(None)
    maps = make_in_maps(inputs)
    res = run_bass_kernel_spmd(nc, maps, core_ids=list(range(8)))
    out = np.stack([np.asarray(r["out"], np.float32) for r in res.results], axis=0)
    return out
```

```python
import contextlib
import numpy as np
import ml_dtypes
import concourse.bass as bass
import concourse.mybir as mybir
from concourse.bass_utils import run_bass_kernel_spmd

dt = mybir.dt
AF = mybir.ActivationFunctionType
ALU = mybir.AluOpType
F32, BF16 = dt.float32, dt.bfloat16
ISZ = {dt.float32: 4, dt.bfloat16: 2, dt.int32: 4, dt.uint32: 4, dt.float16: 2}

D = 1024
S = 2048
DFF = 2816
KC = D // 128
NT = S // 128
NB = S // 512
EPS = 1e-6
O_RQ, O_RK, O_RV, O_RG, O_DQ, O_DK, O_DV, O_G0, O_G1 = 0, 1024, 2048, 4096, 6144, 7168, 8192, 9216, 10240
LAM_INIT = 0.8 - 0.6 * float(np.exp(-0.3 * 0))
NEG = -30000.0

ENGS = ("pe", "act", "dve", "pool", "sp")
SB_G = 128


class Prog:
    def __init__(self, nc):
        self.nc = nc
        self.ops = []
        self.sb_off = (int(nc.sbuf_base) + 255) // 256 * 256
        self.sb_top = int(nc.sbuf_top)
        self.tinfo = {}
        self.psum_banks = 0
        self.cells = {}

    def sb(self, name, shape, dtype, at=None):
        isz = ISZ[dtype]
        free = int(np.prod(shape[1:]))
        nbytes = free * isz
        if at is None:
            at = self.sb_off
            self.sb_off = (at + nbytes + 63) // 64 * 64
        assert at + nbytes <= self.sb_top, f"SBUF overflow at {name}: {at + nbytes} > {self.sb_top}"
        h = self.nc.alloc_sbuf_tensor_at(name, list(shape), dtype, offset=at)
        self.tinfo[h.name] = ("sb", at, free, isz)
        return h

    def ps(self, name):
        h = self.nc.alloc_psum_tensor(name, [128, 512], F32)
        self.tinfo[h.name] = ("ps", self.psum_banks, 512, 4)
        self.psum_banks += 1
        return h

    def cells_of(self, ap):
        name = ap.tensor.name
        info = self.tinfo.get(name)
        if info is None:
            return []
        space, base, free, isz0 = info
        if space == "ps":
            if free * isz0 <= 2048:
                return [("ps", base)]
            isz_ = ISZ[ap.dtype]
            pstr = free * isz0 // isz_
            e0_ = int(ap.offset) % pstr
            ext_ = 0
            for st, cnt in ap.ap[1:]:
                ext_ += (cnt - 1) * abs(st)
            lo_ = e0_ * isz_
            hi_ = (e0_ + ext_ + 1) * isz_
            return [("ps", base + b) for b in range(lo_ // 2048, (hi_ - 1) // 2048 + 1)]
        isz = ISZ[ap.dtype]
        pstride = free * isz0 // isz
        off = int(ap.offset)
        p0 = off // pstride
        e0 = off % pstride
        dims = ap.ap
        npart = dims[0][1]
        ext = 0
        for st, cnt in dims[1:]:
            ext += (cnt - 1) * abs(st)
        lo = base + e0 * isz
        hi = base + (e0 + ext + 1) * isz
        out = []
        for q in range(p0 // 32, (p0 + npart - 1) // 32 + 1):
            for c in range(lo // SB_G, (hi - 1) // SB_G + 1):
                out.append(("sb", q, c))
        return out

    def op(self, eng, fn, reads=(), writes=(), dma=None, dreads=(), dwrites=()):
        idx = len(self.ops)
        deps = set()
        rc, wc = [], []
        for a in reads:
            rc.extend(self.cells_of(a))
        for a in writes:
            wc.extend(self.cells_of(a))
        rc.extend(("dram", k) for k in dreads)
        wc.extend(("dram", k) for k in dwrites)
        ops = self.ops
        for c in rc:
            st = self.cells.get(c)
            if st is not None:
                if st[0] is not None:
                    deps.add(st[0])
                if c[0] == "ps":
                    for r in st[1].values():
                        if ops[r]["eng"] != eng:
                            deps.add(r)
        for c in wc:
            st = self.cells.get(c)
            if st is not None:
                if st[0] is not None:
                    deps.add(st[0])
                deps.update(st[1].values())
        rkey = ("d", idx) if dma is not None else eng
        for c in rc:
            st = self.cells.get(c)
            if st is None:
                st = [None, {}]
                self.cells[c] = st
            st[1][rkey] = idx
        for c in wc:
            st = self.cells.get(c)
            if st is None:
                st = [None, {}]
                self.cells[c] = st
            st[0] = idx
            st[1] = {}
        deps.discard(idx)
        self.ops.append(dict(eng=eng, fn=fn, deps=deps, dma=dma))
        return idx

    def emit(self):
        nc = self.nc
        ops = self.ops
        needs = [False] * len(ops)
        for o in ops:
            keep = set()
            latest = {}
            for d in o["deps"]:
                p = ops[d]
                if p["dma"] is not None:
                    keep.add(d)
                    continue
                if p["eng"] == "pe" and o["eng"] == "pe":
                    continue
                if d > latest.get(p["eng"], -1):
                    latest[p["eng"]] = d
            keep.update(latest.values())
            o["deps"] = keep
            for d in keep:
                needs[d] = True
        eng_cnt = {e: 0 for e in ENGS}
        dma_cnt = {}
        sig = [None] * len(ops)
        for i, o in enumerate(ops):
            if o["dma"] is not None:
                k = ("dma", o["dma"])
                dma_cnt[k] = dma_cnt.get(k, 0) + 16
                sig[i] = (k, dma_cnt[k])
            elif needs[i]:
                eng_cnt[o["eng"]] += 1
                sig[i] = (("eng", o["eng"]), eng_cnt[o["eng"]])
        semkeys = [("eng", e) for e in ENGS] + sorted(dma_cnt.keys())
        per_eng = {e: [] for e in ENGS}
        for i, o in enumerate(ops):
            per_eng[o["eng"]].append(i)
        self.stats = dict(n_ops=len(ops), per_eng={e: len(v) for e, v in per_eng.items()},
                          eng_cnt=dict(eng_cnt), n_sems=len(semkeys), waits={})
        with contextlib.ExitStack() as es:
            sems = {}
            for k in semkeys:
                sems[k] = es.enter_context(nc.semaphore("s_" + "_".join(str(x) for x in k)))
            block = es.enter_context(nc.Block())

            def run(engname, eng):
                waited = {}
                nwait = 0
                for i in per_eng[engname]:
                    o = ops[i]
                    req = {}
                    for d in o["deps"]:
                        k, v = sig[d]
                        if v > req.get(k, 0):
                            req[k] = v
                    for k, v in req.items():
                        if waited.get(k, 0) >= v:
                            continue
                        eng.wait_ge(sems[k], v)
                        waited[k] = v
                        nwait += 1
                    ins = o["fn"](eng)
                    if sig[i] is not None:
                        k, v = sig[i]
                        ins.then_inc(sems[k], 16 if k[0] == "dma" else 1)
                self.stats["waits"][engname] = nwait

            @block.tensor
            def _(e):
                run("pe", e)

            @block.scalar
            def _(e):
                run("act", e)

            @block.vector
            def _(e):
                run("dve", e)

            @block.gpsimd
            def _(e):
                run("pool", e)

            @block.sync
            def _(e):
                run("sp", e)


class K:
    def __init__(self, nc, stage=None):
        self.nc = nc
        self.stage = stage
        self.P = Prog(nc)
        self.rr = {}

    def mm(self, out, lhsT, rhs, start, stop, skip=False):
        if skip:
            self.P.op("pe", lambda e: e.matmul(out, lhsT=lhsT, rhs=rhs, start=start, stop=stop,
                                               skip_group_check=True),
                      reads=[lhsT, rhs], writes=[out])
        else:
            self.P.op("pe", lambda e: e.matmul(out, lhsT=lhsT, rhs=rhs, start=start, stop=stop),
                      reads=[lhsT, rhs], writes=[out])

    def tr(self, out, in_, ident):
        self.P.op("pe", lambda e: e.transpose(out=out, in_=in_, identity=ident),
                  reads=[in_, ident], writes=[out])

    def act(self, out, in_, func, scale=1.0, bias=None, accum=None):
        reads = [in_]
        kw = {}
        if not isinstance(scale, (int, float)):
            reads.append(scale)
        if bias is not None:
            kw["bias"] = bias
            if not isinstance(bias, (int, float)):
                reads.append(bias)
        writes = [out]
        if accum is not None:
            kw["accum_out"] = accum
            writes.append(accum)
        self.P.op("act", lambda e: e.activation(out=out, in_=in_, func=func, scale=scale, **kw),
                  reads=reads, writes=writes)

    def tt(self, eng, out, in0, in1, op):
        self.P.op(eng, lambda e: e.tensor_tensor(out=out, in0=in0, in1=in1, op=op),
                  reads=[in0, in1], writes=[out])

    def ts(self, eng, out, in0, s1, op0, s2=None, op1=None):
        reads = [in0] + [s for s in (s1, s2) if s is not None and not isinstance(s, (int, float))]
        if op1 is None:
            self.P.op(eng, lambda e: e.tensor_scalar(out=out, in0=in0, scalar1=s1, scalar2=None, op0=op0),
                      reads=reads, writes=[out])
        else:
            self.P.op(eng, lambda e: e.tensor_scalar(out=out, in0=in0, scalar1=s1, scalar2=s2, op0=op0, op1=op1),
                      reads=reads, writes=[out])

    def stt(self, eng, out, in0, scalar, in1, op0, op1):
        reads = [in0, in1] + ([] if isinstance(scalar, (int, float)) else [scalar])
        self.P.op(eng, lambda e: e.scalar_tensor_tensor(out=out, in0=in0, scalar=scalar, in1=in1, op0=op0, op1=op1),
                  reads=reads, writes=[out])

    def cp(self, eng, out, in_):
        if eng == "act":
            self.act(out, in_, AF.Copy)
        else:
            self.P.op(eng, lambda e: e.tensor_copy(out=out, in_=in_), reads=[in_], writes=[out])

    def memset(self, eng, ap, val):
        self.P.op(eng, lambda e: e.memset(ap, val), writes=[ap])

    def dma(self, q, out, in_, sem, reads=(), writes=(), dreads=(), dwrites=()):
        self.P.op(q, lambda e: e.dma_start(out=out, in_=in_), reads=list(reads), writes=list(writes),
                  dma=sem, dreads=dreads, dwrites=dwrites)

    def rot(self, key, n):
        v = self.rr.get(key, 0)
        self.rr[key] = v + 1
        return v % n

    def wload(self, ring, parts, after=()):
        slots = self.wA if ring == "A" else self.wB
        i = self.rot("w" + ring, len(slots))
        slot = slots[i]
        off = 0
        views = []
        for pi, (src, kc, ncols) in enumerate(parts):
            n = kc * ncols
            v = slot[:, off:off + n].rearrange("p (k c) -> p k c", k=kc)
            self.dma("pool", v, src.rearrange("(k p) c -> p k c", p=128), f"w{ring}{i}_{pi}", writes=[v],
                     dreads=after)
            views.append(v)
            off += n
        assert off <= slot.shape[1]
        return views

    def build(self):
        nc, P = self.nc, self.P
        stage = self.stage
        din = lambda name, shape: nc.dram_tensor(name, list(shape), F32, kind="ExternalInput").ap()
        x_d = din("x", [S, D])
        c_d = din("c", [128, KC])
        wcond_d = din("w_cond", [D, 9 * D])
        bcond_d = din("b_cond", [128, 72])
        gnorm_d = din("g_norm", [128, 24])
        wf_in = [din("w_ffn1_in", [D, 2 * DFF]), din("w_ffn2_in", [D, 2 * DFF])]
        wf_out = [din("w_ffn1_out", [DFF, D]), din("w_ffn2_out", [DFF, D])]
        win_d = din("w_in", [D, 11264])
        wro_d = din("w_ret_out", [2048, D])
        lam_d = din("diff_lambda", [256])
        subln_d = din("diff_subln", [128])
        wdo_d = din("w_diff_out", [D, D])
        wout_d = din("w_out", [D, D])
        gfin_d = din("g_final", [128, KC])
        ident_d = din("c_ident", [128, 128])
        kdec_d = din("c_kdec", [4, 128, 512])
        xi_d = din("c_xi", [128, 4])
        m01_d = din("c_mask01", [128, 128])
        um_d = din("c_umask", [128, 128])
        qx_d = nc.dram_tensor("c_qx", [8, 6, S], BF16, kind="ExternalInput").ap()
        kx_d = nc.dram_tensor("c_kx", [8, 6, S], BF16, kind="ExternalInput").ap()
        out_d = nc.dram_tensor("out", [S, D], F32, kind="ExternalOutput").ap()
        spk = "ExternalOutput" if stage in ("ret", "diff") else "Internal"
        ret_sp = nc.dram_tensor("ret_sp", [8, 128, 16, 256], BF16, kind=spk).ap()
        diff_sp = nc.dram_tensor("diff_sp", [8, 128, 8, 256], BF16, kind=spk).ap()
        if stage is not None:
            dbg_x = nc.dram_tensor("dbg_x", [128, KC, S], F32, kind="ExternalOutput").ap()
            dbg_h = nc.dram_tensor("dbg_h", [128, KC, S], BF16, kind="ExternalOutput").ap()
            dbg_m = nc.dram_tensor("dbg_m", [128, 80], F32, kind="ExternalOutput").ap()
        self.x_d, self.out_d = x_d, out_d

        self.xT = xT = P.sb("xT", [128, KC, S], F32)
        self.hT = hT = P.sb("hT", [128, KC, S], BF16)
        self.wA = [P.sb(f"wA{i}", [128, 8192], BF16) for i in range(2)]
        self.wB = [P.sb(f"wB{i}", [128, 4096], BF16) for i in range(2)]
        id32 = P.sb("id32", [128, 128], F32)
        idb = P.sb("idb", [128, 128], BF16)
        ones32 = P.sb("ones32", [128, 128], F32)
        nhalf = P.sb("nhalf", [128, 512], F32)
        m01 = P.sb("m01", [128, 128], F32)
        umask = P.sb("umask", [128, 128], BF16)
        sublnB = P.sb("sublnB", [128, 128], F32)
        kdec = P.sb("kdec", [128, 4, 512], F32)
        xi = P.sb("xi", [128, 4], F32)
        c_sb = P.sb("c_sb", [128, KC], F32)
        c_bf = P.sb("c_bf", [128, KC], BF16)
        self._c_bf = c_bf
        bcond = P.sb("bcond", [128, 72], F32)
        gnorm = P.sb("gnorm", [128, 24], F32)
        gfin = P.sb("gfin", [128, KC], F32)
        modT = P.sb("modT", [128, 72], F32)
        gs = P.sb("gs", [128, 24], F32)
        gatef = P.sb("gatef", [128, 24], F32)
        lamb = P.sb("lamb", [128, 256], F32)
        lsm = P.sb("lsm", [128, 72], F32)
        neglam = P.sb("neglam", [128, 1], F32)
        self.id32, self.idb, self.ones32, self.nhalf = id32, idb, ones32, nhalf
        self.gs, self.gatef, self.modT = gs, gatef, modT
        arena = (P.sb_off + 255) // 256 * 256
        self.arena = arena
        asz = P.sb_top - arena
        self.asz = asz
        ps_lo = [P.ps(f"ps{i}") for i in range(2)]
        self.psbig = nc.alloc_psum_tensor("psbig", [128, 2048], F32)
        P.tinfo[self.psbig.name] = ("ps", 2, 2048, 4)
        P.psum_banks += 4
        ps_hi = [P.ps(f"ps{i}") for i in (6, 7)]
        self.ps = ps_lo + [self.psbig[:, 512 * j:512 * (j + 1)] for j in range(4)] + ps_hi
        ps = self.ps

        sp = "sp"
        self.dma(sp, id32[:, :], ident_d, "c0", writes=[id32[:, :]])
        self.dma("pool", idb[:, :], ident_d, "c1", writes=[idb[:, :]])
        self.dma(sp, m01[:, :], m01_d, "c2", writes=[m01[:, :]])
        self.dma("pool", umask[:, :], um_d, "c3", writes=[umask[:, :]])
        self.dma(sp, kdec[:, :, :], kdec_d.rearrange("h p t -> p h t"), "c4", writes=[kdec[:, :, :]])
        self.dma(sp, xi[:, :], xi_d, "c5", writes=[xi[:, :]])
        self.dma(sp, c_sb[:, :], c_d, "c6", writes=[c_sb[:, :]])
        self.dma(sp, bcond[:, :], bcond_d, "c7", writes=[bcond[:, :]])
        self.dma(sp, gnorm[:, :], gnorm_d, "c8", writes=[gnorm[:, :]])
        self.dma(sp, gfin[:, :], gfin_d, "c9", writes=[gfin[:, :]])
        self.dma(sp, lamb[:, :], lam_d.partition_broadcast(128), "c10", writes=[lamb[:, :]])
        self.dma(sp, sublnB[:, :], subln_d.partition_broadcast(128), "c11", writes=[sublnB[:, :]])
        self.memset("dve", ones32[:, :], 1.0)
        self.memset("dve", nhalf[:, :], -0.5)
        self.ts("dve", sublnB[:, :], sublnB[:, :], 1.0 - LAM_INIT, ALU.mult)
        self.act(c_bf[:, :], c_sb[:, :], AF.Silu)
        self.tt("dve", lsm[:, 0:64], lamb[:, 0:64], lamb[:, 64:128], ALU.mult)
        P.op("dve", lambda e: e.reduce_sum(out=lsm[:, 64:65], in_=lsm[:, 0:64], axis=mybir.AxisListType.X),
             reads=[lsm[:, 0:64]], writes=[lsm[:, 64:65]])
        self.tt("dve", lsm[:, 0:64], lamb[:, 128:192], lamb[:, 192:256], ALU.mult)
        P.op("dve", lambda e: e.reduce_sum(out=lsm[:, 65:66], in_=lsm[:, 0:64], axis=mybir.AxisListType.X),
             reads=[lsm[:, 0:64]], writes=[lsm[:, 65:66]])
        self.act(lsm[:, 66:68], lsm[:, 64:66], AF.Exp)
        self.tt("dve", lsm[:, 68:69], lsm[:, 67:68], lsm[:, 66:67], ALU.subtract)
        self.ts("dve", neglam[:, :], lsm[:, 68:69], -LAM_INIT, ALU.add)

        xs = [P.sb(f"xs{i}", [128, D], F32, at=arena + i * 4096) for i in range(4)]
        for tt_ in range(NT):
            b = tt_ % 4
            self.dma(sp, xs[b][:, :], x_d[tt_ * 128:(tt_ + 1) * 128, :], f"xin{b}", writes=[xs[b][:, :]],
                     dwrites=[("xk", tt_)])
            for half in range(2):
                bank = ps[self.rot("psx", 4)]
                for j in range(4):
                    dc = half * 4 + j
                    self.tr(bank[:, j * 128:(j + 1) * 128], xs[b][:, dc * 128:(dc + 1) * 128], id32[:, :])
                dst = xT[:, half * 4:half * 4 + 4, tt_ * 128:(tt_ + 1) * 128]
                src = bank[:, :].rearrange("p (a b) -> p a b", a=4)
                self.cp("act" if half == 0 else "dve", dst, src)

        self.wC = P.sb("wC", [128, 4096], BF16, at=arena + 34816)
        self.norm_stats(0)
        self.mod_groups(wcond_d, bcond, gnorm, 0)
        self.norm_apply(0)
        if stage == "norm0":
            return self.finish_debug(dbg_x, dbg_h, dbg_m)
        halves = [(g, hf) for g in range(3, 9) for hf in range(2)]

        def extra(gi):
            for (g, hf) in halves[2 * gi:2 * gi + 2]:
                self.mod_half(wcond_d, g, hf)
        def extra2(gi):
            extra(gi)
            if gi == 5:
                self.mod_finish(bcond, gnorm, 1)
                self.mod_finish(bcond, gnorm, 2)
        self.ffn(0, wf_in[0], wf_out[0], extra=extra2, tail=lambda tb: self.norm_block(1, tb))
        if stage == "ffn1":
            return self.finish_debug(dbg_x, dbg_h, dbg_m)
        self.retention(win_d, kdec, xi, m01, ret_sp)
        if stage == "ret":
            return self.finish_debug(dbg_x, dbg_h, dbg_m)
        self.diffattn(win_d, qx_d, kx_d, umask, sublnB, neglam, diff_sp)
        if stage == "diff":
            return self.finish_debug(dbg_x, dbg_h, dbg_m)
        self.merge(win_d, wro_d, wdo_d, wout_d, ret_sp, diff_sp, tail=lambda tb: self.norm_block(2, tb))
        if stage == "mix":
            return self.finish_debug(dbg_x, dbg_h, dbg_m)
        self.gfin = gfin
        self.yf = [P.sb(f"yf{k}", [128, 512], F32, at=arena + k * 2048) for k in range(8)]
        self.ost = [P.sb(f"ost{k}", [128, D], F32, at=arena + 34816 + 6144 + k * 4096) for k in range(2)]
        self.ffn(2, wf_in[1], wf_out[1], tail=lambda tb: self.norm_block(3, tb))
        self.final_out(gfin)
        P.emit()
        return P

    def finish_debug(self, dbg_x, dbg_h, dbg_m):
        xT, hT = self.xT, self.hT
        self.dma("sp", dbg_x, xT[:, :, :], "dbg0", reads=[xT[:, :, :]], dwrites=["dbgx"])
        self.dma("sp", dbg_h, hT[:, :, :], "dbg1", reads=[hT[:, :, :]], dwrites=["dbgh"])
        self.dma("sp", dbg_m[:, 0:72], self.modT[:, :], "dbg2", reads=[self.modT[:, :]], dwrites=["dbgm"])
        keys = ["dbgx", "dbgh", "dbgm"]
        if self.stage in ("ret", "diff"):
            keys += [("ret_sp", hh, tb) for hh in range(4) for tb in range(8)]
        if self.stage == "diff":
            keys += [("diff_sp", hh) for hh in range(8)]
        self.P.op("sp", lambda e: e.nop(), dreads=keys)
        self.P.emit()
        return self.P

    def mod_groups(self, wcond_d, bcond, gnorm, i):
        ps, modT, gs, gatef = self.ps, self.modT, self.gs, self.gatef
        bank = ps[7]
        for j in range(3):
            g = i * 3 + j
            (w,) = self.wload("A", [(wcond_d[:, g * D:(g + 1) * D], KC, D)],
                              after=[("xk", 9)] if (i == 0 and j == 0) else ())
            for fc in range(KC):
                col = g * 8 + fc
                for kc in range(KC):
                    self.mm(bank[:, col:col + 1], w[:, kc, fc * 128:(fc + 1) * 128], self.c_bf_ap(kc),
                            kc == 0, kc == KC - 1)
        c0 = i * 24
        self.tt("dve", modT[:, c0:c0 + 24], bank[:, c0:c0 + 24], bcond[:, c0:c0 + 24], ALU.add)
        self.stt("dve", gs[:, i * 8:i * 8 + 8], modT[:, c0 + 8:c0 + 16], 1.0, gnorm[:, i * 8:i * 8 + 8],
                 ALU.add, ALU.mult)
        self.ts("dve", gatef[:, i * 8:i * 8 + 8], modT[:, c0 + 16:c0 + 24], 1.0 if i == 1 else 0.5, ALU.mult)

    def mod_half(self, wcond_d, g, half):
        P, ps = self.P, self.ps
        slot = self.wC
        v = slot[:, :].rearrange("p (k c) -> p k c", k=KC)
        src = wcond_d[:, g * D + half * 512:g * D + (half + 1) * 512]
        self.dma("pool", v, src.rearrange("(k p) c -> p k c", p=128), "wC", writes=[v])
        bank = ps[7]
        for f4 in range(4):
            col = g * 8 + half * 4 + f4
            for kc in range(KC):
                self.mm(bank[:, col:col + 1], v[:, kc, f4 * 128:(f4 + 1) * 128], self.c_bf_ap(kc),
                        kc == 0, kc == KC - 1)

    def mod_finish(self, bcond, gnorm, i):
        ps, modT, gs, gatef = self.ps, self.modT, self.gs, self.gatef
        bank = ps[7]
        c0 = i * 24
        self.tt("dve", modT[:, c0:c0 + 24], bank[:, c0:c0 + 24], bcond[:, c0:c0 + 24], ALU.add)
        self.stt("dve", gs[:, i * 8:i * 8 + 8], modT[:, c0 + 8:c0 + 16], 1.0, gnorm[:, i * 8:i * 8 + 8],
                 ALU.add, ALU.mult)
        self.ts("dve", gatef[:, i * 8:i * 8 + 8], modT[:, c0 + 16:c0 + 24], 1.0 if i == 1 else 0.5, ALU.mult)

    def c_bf_ap(self, kc):
        return self._c_bf[:, kc:kc + 1]

    def norm_modulate(self, i):
        self.norm_stats(i)
        self.norm_apply(i)

    def norm_bufs(self, i):
        P, A = self.P, self.arena
        sq = [P.sb(f"sq{i}_{k}", [128, 512], F32, at=A + k * 2048) for k in range(2)]
        vt = [P.sb(f"vt{i}_{k}", [128, 512], F32, at=A + 4096 + k * 2048) for k in range(2)]
        rstd = [P.sb(f"rstd{i}_{k}", [128, 512], F32, at=A + 8192 + k * 2048) for k in range(4)]
        return sq, vt, rstd

    def norm_stats(self, i):
        ps, xT = self.ps, self.xT
        sq, vt, rstd = self.norm_bufs(i)
        self._rstd = rstd
        for tb in range(NB):
            tsl = slice(tb * 512, (tb + 1) * 512)
            bank = ps[4 + self.rot("psn", 2)]
            for dc in range(KC):
                s_ = sq[self.rot("sq", 2)]
                self.act(s_[:, :], xT[:, dc, tsl], AF.Square)
                self.mm(bank[:, :], self.ones32[:, :], s_[:, :], dc == 0, dc == KC - 1)
            v = vt[tb % 2]
            self.ts("dve", v[:, :], bank[:, :], 1.0 / D, ALU.mult, EPS, ALU.add)
            self.act(v[:, :], v[:, :], AF.Sqrt)
            r = rstd[tb]
            self.P.op("dve", lambda e, r=r, v=v: e.reciprocal(out=r[:, :], in_=v[:, :]),
                      reads=[v[:, :]], writes=[r[:, :]])

    def norm_block(self, i, tb):
        P, ps, xT, hT = self.P, self.ps, self.xT, self.hT
        B0 = self.arena + 34816
        sq = [P.sb(f"nbsq{i}_{tb}_{k}", [128, 512], F32, at=B0 + k * 2048) for k in range(2)]
        if i == 3:
            v = P.sb(f"nbr{i}_{tb}", [128, 512], F32, at=B0 + 4096)
        else:
            v = P.sb(f"nbv{i}_{tb}", [128, 512], F32, at=B0 + 4096)
            tmp = [P.sb(f"nbt{i}_{tb}_{k}", [128, 512], F32, at=B0 + 6144 + k * 2048) for k in range(2)]
        tsl = slice(tb * 512, (tb + 1) * 512)
        bank = ps[6 + self.rot("psnb", 2)]
        for dc in range(KC):
            s_ = sq[self.rot("sq", 2)]
            self.act(s_[:, :], xT[:, dc, tsl], AF.Square)
            self.mm(bank[:, :], self.ones32[:, :], s_[:, :], dc == 0, dc == KC - 1)
        self.ts("dve", v[:, :], bank[:, :], 1.0 / D, ALU.mult, EPS, ALU.add)
        self.act(v[:, :], v[:, :], AF.Sqrt)
        self.P.op("dve", lambda e: e.reciprocal(out=v[:, :], in_=v[:, :]), reads=[v[:, :]], writes=[v[:, :]])
        if i == 3:
            self.final_block(tb, v)
            return
        for dc in range(KC):
            t = tmp[self.rot("ntmp", 2)]
            self.stt("dve", t[:, :], xT[:, dc, tsl], self.gs[:, i * 8 + dc:i * 8 + dc + 1], v[:, :],
                     ALU.mult, ALU.mult)
            self.act(hT[:, dc, tsl], t[:, :], AF.Identity, bias=self.modT[:, i * 24 + dc:i * 24 + dc + 1])

    def norm_apply(self, i):
        P, A, xT, hT = self.P, self.arena, self.xT, self.hT
        rstd = self._rstd
        tmp = [P.sb(f"ntmp{i}_{k}", [128, 512], F32, at=A + 16384 + k * 2048) for k in range(2)]
        for tb in range(NB):
            tsl = slice(tb * 512, (tb + 1) * 512)
            r = rstd[tb]
            if i == 3:
                self.final_block(tb, r)
                continue
            for dc in range(KC):
                t = tmp[self.rot("ntmp", 2)]
                self.stt("dve", t[:, :], xT[:, dc, tsl], self.gs[:, i * 8 + dc:i * 8 + dc + 1], r[:, :],
                         ALU.mult, ALU.mult)
                self.act(hT[:, dc, tsl], t[:, :], AF.Identity, bias=self.modT[:, i * 24 + dc:i * 24 + dc + 1])

    def ffn(self, i, w_in_d, w_out_d, extra=None, tail=None):
        P, ps, xT, hT = self.P, self.ps, self.xT, self.hT
        A = self.arena
        actb = [P.sb(f"actb{i}_{k}", [128, 4, S], BF16, at=A + k * 16384) for k in range(2)]
        sa = [P.sb(f"sa{i}_{k}", [128, 512], BF16, at=A + 32768 + k * 1024) for k in range(2)]
        groups = [(g * 512, 512) for g in range(5)] + [(2560, 256)]
        for gi, (f0, fw) in enumerate(groups):
            nf = fw // 128
            wa, wb = self.wload("A", [(w_in_d[:, f0:f0 + fw], KC, fw),
                                      (w_in_d[:, DFF + f0:DFF + f0 + fw], KC, fw)])
            (wo,) = self.wload("B", [(w_out_d[f0:f0 + fw, :], nf, D)])
            ab = actb[gi % 2]
            for tb in range(NB):
                tsl = slice(tb * 512, (tb + 1) * 512)
                for ft in range(nf):
                    pa = ps[self.rot("ffa", 2)]
                    pb = ps[2 + self.rot("ffb", 2)]
                    for kc in range(KC):
                        self.mm(pa[:, :], wa[:, kc, ft * 128:(ft + 1) * 128], hT[:, kc, tsl], kc == 0, kc == KC - 1)
                    for kc in range(KC):
                        self.mm(pb[:, :], wb[:, kc, ft * 128:(ft + 1) * 128], hT[:, kc, tsl], kc == 0, kc == KC - 1)
                    s = sa[self.rot("sa", 2)]
                    self.act(s[:, :], pa[:, :], AF.Silu)
                    self.tt("dve", ab[:, ft, tsl], pb[:, :], s[:, :], ALU.mult)
            if extra is not None:
                extra(gi)
            for tb in range(NB):
                tsl = slice(tb * 512, (tb + 1) * 512)
                for dc in range(KC):
                    po = ps[4 + self.rot("ffo", 2)]
                    for ft in range(nf):
                        self.mm(po[:, :], wo[:, ft, dc * 128:(dc + 1) * 128], ab[:, ft, tsl], ft == 0, ft == nf - 1)
                    self.stt("dve", xT[:, dc, tsl], po[:, :], self.gatef[:, i * 8 + dc:i * 8 + dc + 1],
                             xT[:, dc, tsl], ALU.mult, ALU.add)
                if tail is not None and gi == len(groups) - 1 and tb > 0:
                    tail(tb - 1)
        if tail is not None:
            tail(NB - 1)

    def retention(self, win_d, kdec, xi, m01, ret_sp):
        P, ps, hT = self.P, self.ps, self.hT
        A = self.arena
        o = [0]

        def al(name, shape, dtype):
            n = int(np.prod(shape[1:])) * ISZ[dtype]
            t = P.sb(name, shape, dtype, at=A + o[0])
            o[0] += (n + 63) // 64 * 64
            assert o[0] <= self.asz, ("ret arena", o[0], self.asz)
            return t
        qT = [al(f"r_qT{k}", [128, 2, 512], BF16) for k in range(2)]
        kT = [al(f"r_kT{k}", [128, 2, 512], BF16) for k in range(2)]
        vb = [al(f"r_v{k}", [128, 4, 512], BF16) for k in range(2)]
        sg = [al(f"r_sg{k}", [128, 4, 512], BF16) for k in range(2)]
        R32 = al("r_R32", [128, 2, 512], F32)
        Rbf = al("r_Rbf", [128, 2, 512], BF16)
        k2 = [al(f"r_k2{k}", [128, 256], BF16) for k in range(2)]
        PT = [al(f"r_PT{k}", [128, 128], BF16) for k in range(2)]
        osb = [al(f"r_osb{k}", [128, 512], F32) for k in range(2)]
        nbf = [al(f"r_nbf{k}", [128, 512], BF16) for k in range(2)]
        gtd = [al(f"r_gtd{k}", [128, 512], BF16) for k in range(2)]
        gst = [al(f"r_gst{k}", [128, 4, 512], BF16) for k in range(2)]
        st6 = [al(f"r_st{k}", [128, 6], F32) for k in range(2)]
        mv = [al(f"r_mv{k}", [128, 8], F32) for k in range(2)]
        psT6 = ps[6][:, :].bitcast(BF16)
        psT7 = ps[7][:, :].bitcast(BF16)
        W = {}

        def load_head(h):
            wq, wk, wv = self.wload("A", [(win_d[:, O_RQ + h * 256:O_RQ + (h + 1) * 256], KC, 256),
                                          (win_d[:, O_RK + h * 256:O_RK + (h + 1) * 256], KC, 256),
                                          (win_d[:, O_RV + h * 512:O_RV + (h + 1) * 512], KC, 512)])
            (wg,) = self.wload("B", [(win_d[:, O_RG + h * 512:O_RG + (h + 1) * 512], KC, 512)])
            W[h] = (wq, wk, wv, wg)

        def proj_piece(bidx, c):
            h, tb = divmod(bidx, 4)
            wq, wk, wv, wg = W[h]
            bi = bidx % 2
            tsl = slice(tb * 512, (tb + 1) * 512)
            bank = ps[self.rot("rpj", 2)]
            if c < 2:
                for kc in range(KC):
                    self.mm(bank[:, :], wq[:, kc, c * 128:(c + 1) * 128], hT[:, kc, tsl], kc == 0, kc == KC - 1)
                self.cp("act", qT[bi][:, c, :], bank[:, :])
            else:
                dcq = c - 2
                for kc in range(KC):
                    self.mm(bank[:, :], wk[:, kc, dcq * 128:(dcq + 1) * 128], hT[:, kc, tsl], kc == 0, kc == KC - 1)
                self.tt("dve", kT[bi][:, dcq, :], bank[:, :], kdec[:, h, :], ALU.mult)
            tok = slice(tb * 512 + c * 128, tb * 512 + (c + 1) * 128)
            bank = ps[self.rot("rpj", 2)]
            for kc in range(KC):
                self.mm(bank[:, :], hT[:, kc, tok], wv[:, kc, :], kc == 0, kc == KC - 1)
            self.cp("act", vb[bi][:, c, :], bank[:, :])
            bank = ps[self.rot("rpj", 2)]
            for kc in range(KC):
                self.mm(bank[:, :], hT[:, kc, tok], wg[:, kc, :], kc == 0, kc == KC - 1)
            self.act(sg[bi][:, c, :], bank[:, :], AF.Silu)

        def tg(bidx, c):
            h, tb = divmod(bidx, 4)
            bi = bidx % 2
            j = (bidx * 4 + c) % 2
            cs = slice(c * 128, (c + 1) * 128)
            for e4 in range(4):
                self.tr(psT7[:, e4 * 128:(e4 + 1) * 128], gtd[j][:, e4 * 128:(e4 + 1) * 128], self.idb[:, :])
            self.cp("act", gst[bi][:, :, cs], psT7[:, 0:512].rearrange("p (a b) -> p a b", a=4))
            if c == 3:
                for half in range(2):
                    src = gst[bi][:, :, half * 256:(half + 1) * 256]
                    self.dma("sp", ret_sp[2 * tb + half, :, h * 4:(h + 1) * 4, :], src, f"rsp{bi}{half}",
                             reads=[src], dwrites=[("ret_sp", h, 2 * tb + half)])

        def e2(bidx, c):
            bi = bidx % 2
            j = (bidx * 4 + c) % 2
            m = mv[j]
            self.act(nbf[j][:, :], osb[j][:, :], AF.Identity, scale=m[:, 3:4], bias=m[:, 4:5])
            self.tt("dve", gtd[j][:, :], nbf[j][:, :], sg[bi][:, c, :], ALU.mult)

        load_head(0)
        for c in range(4):
            proj_piece(0, c)
        pend1 = None
        pend2 = None
        for bidx in range(16):
            h, tb = divmod(bidx, 4)
            bi = bidx % 2
            gam = 1.0 - 2.0 ** (-5 - h)
            gC = float(gam ** 128)
            if tb == 0:
                if h + 1 < 4:
                    load_head(h + 1)
                self.memset("pool", R32[:, :, :], 0.0)
            for c in range(4):
                n = tb * 4 + c
                cs = slice(c * 128, (c + 1) * 128)
                j = (bidx * 4 + c) % 2
                last = (n == NT - 1)
                for dcq in range(2):
                    self.mm(ps[2][:, 0:128], kT[bi][:, dcq, cs], qT[bi][:, dcq, cs], dcq == 0, dcq == 1)
                self.tt("dve", PT[j][:, :], ps[2][:, 0:128], m01[:, :], ALU.mult)
                if not last:
                    for dcq in range(2):
                        self.tr(psT6[:, dcq * 128:(dcq + 1) * 128], kT[bi][:, dcq, cs], self.idb[:, :])
                    self.act(k2[j][:, :], psT6[:, 0:256], AF.Copy, scale=gC)
                if bidx + 1 < 16:
                    proj_piece(bidx + 1, c)
                new_e2 = None
                if pend1 is not None:
                    e2(*pend1)
                    new_e2 = pend1
                    pend1 = None
                self.mm(ps[3][:, :], PT[j][:, :], vb[bi][:, c, :], True, n == 0)
                if n > 0:
                    for dcq in range(2):
                        self.mm(ps[3][:, :], qT[bi][:, dcq, cs], Rbf[:, dcq, :], False, dcq == 1)
                if not last:
                    for dcq in range(2):
                        self.mm(ps[4 + dcq][:, :], k2[j][:, dcq * 128:(dcq + 1) * 128], vb[bi][:, c, :], True, True)
                    for dcq in range(2):
                        self.stt("dve", R32[:, dcq, :], R32[:, dcq, :], gC, ps[4 + dcq][:, :], ALU.mult, ALU.add)
                        self.cp("dve", Rbf[:, dcq, :], R32[:, dcq, :])
                if pend2 is not None:
                    tg(*pend2)
                    pend2 = None
                self.act(osb[j][:, :], ps[3][:, :], AF.Identity, scale=xi[:, h:h + 1])
                s6, m = st6[j], mv[j]
                P.op("dve", lambda e, s6=s6, ob=osb[j]: e.bn_stats(out=s6[:, :], in_=ob[:, :]),
                     reads=[osb[j][:, :]], writes=[s6[:, :]])
                P.op("dve", lambda e, s6=s6, m=m: e.bn_aggr(out=m[:, 0:2], in_=s6[:, :]),
                     reads=[s6[:, :]], writes=[m[:, 0:2]])
                self.ts("dve", m[:, 2:3], m[:, 1:2], EPS, ALU.add)
                self.tt("pool", m[:, 3:4], m[:, 2:3], self.nhalf[:, 0:1], ALU.pow)
                self.stt("dve", m[:, 4:5], m[:, 0:1], -1.0, m[:, 3:4], ALU.mult, ALU.mult)
                pend1 = (bidx, c)
                pend2 = new_e2
        e2(*pend1)
        tg(*pend2)
        tg(*pend1)

    def diffattn(self, win_d, qx_d, kx_d, umask, sublnB, neglam, diff_sp):
        P, ps, hT = self.P, self.ps, self.hT
        A = self.arena
        o = [0]

        def al(name, shape, dtype):
            n = int(np.prod(shape[1:])) * ISZ[dtype]
            t = P.sb(name, shape, dtype, at=A + o[0])
            o[0] += (n + 63) // 64 * 64
            assert o[0] <= self.asz, ("diff arena", o[0], self.asz)
            return t
        Q = [al(f"d_Q{m}", [128, S], BF16) for m in range(2)]
        Kt = [al(f"d_K{m}", [128, S], BF16) for m in range(2)]
        V = al("d_V", [128, NT, 132], BF16)
        NP_ = 4
        Psb = [al(f"d_P{k}", [128, 512], BF16) for k in range(NP_)]
        rrb = [al(f"d_rr{k}", [128, 16], F32) for k in range(2)]
        tmpb = [al(f"d_tmp{k}", [128, 2, 2, 128], F32) for k in range(2)]
        osbb = [al(f"d_o{k}", [128, 2, 128], F32) for k in range(2)]
        sqb = [al(f"d_sq{k}", [128, 2, 128], F32) for k in range(2)]
        t2b = [al(f"d_t2{k}", [128, 2, 128], F32) for k in range(2)]
        nrmb = [al(f"d_n{k}", [128, 2, 128], BF16) for k in range(2)]
        dst = [al(f"d_dst{k}", [128, S], BF16) for k in range(2)]
        lamvec = al("d_lamvec", [128, 2], F32)
        psT7 = ps[7][:, :].bitcast(BF16)
        psbig = self.psbig
        self.memset("pool", Q[1][0:64, :], 0.0)
        self.memset("pool", Kt[1][0:64, :], 0.0)
        self.memset("pool", V[:, :, 128:129], 1.0)
        self.memset("dve", lamvec[:, 0:1], 1.0)
        self.cp("dve", lamvec[:, 1:2], neglam[:, :])
        krows = [(0, 70), (0, 128)]
        LAG = 2
        NPAIR = NT // 2

        def sbank():
            return ps[(0, 1, 6)[self.rot("dsc", 3)]]

        def load_w(hh):
            return self.wload("B", [(win_d[:, O_DQ + hh * 128:O_DQ + (hh + 1) * 128], KC, 128),
                                    (win_d[:, O_DK + hh * 128:O_DK + (hh + 1) * 128], KC, 128),
                                    (win_d[:, O_DV + hh * 128:O_DV + (hh + 1) * 128], KC, 128)])
        Wn = load_w(0)
        for h in range(8):
            wq, wk, wv = Wn
            self.dma("sp", Q[0][64:70, :], qx_d[h], "dx0", writes=[Q[0][64:70, :]])
            self.dma("sp", Kt[0][64:70, :], kx_d[h], "dx1", writes=[Kt[0][64:70, :]])
            self.dma("sp", Q[1][0:6, :], qx_d[h], "dx2", writes=[Q[1][0:6, :]])
            self.dma("sp", Kt[1][0:6, :], kx_d[h], "dx3", writes=[Kt[1][0:6, :]])
            for tb in range(NB):
                tsl = slice(tb * 512, (tb + 1) * 512)
                bank = sbank()
                for kc in range(KC):
                    self.mm(bank[:, :], wq[:, kc, :], hT[:, kc, tsl], kc == 0, kc == KC - 1)
                self.ts("dve", Q[0][0:64, tsl], bank[0:64, :], 0.125, ALU.mult)
                self.ts("dve", Q[1][64:128, tsl], bank[64:128, :], 0.125, ALU.mult)
                bank = sbank()
                for kc in range(KC):
                    self.mm(bank[:, :], wk[:, kc, :], hT[:, kc, tsl], kc == 0, kc == KC - 1)
                self.cp("dve", Kt[0][0:64, tsl], bank[0:64, :])
                self.cp("dve", Kt[1][64:128, tsl], bank[64:128, :])
            for t4 in range(4):
                bank = sbank()
                for j in range(4):
                    t_ = t4 * 4 + j
                    tok = slice(t_ * 128, (t_ + 1) * 128)
                    for kc in range(KC):
                        self.mm(bank[:, j * 128:(j + 1) * 128], hT[:, kc, tok], wv[:, kc, :], kc == 0, kc == KC - 1)
                self.cp("dve", V[:, t4 * 4:t4 * 4 + 4, 0:128], bank[:, :].rearrange("p (a b) -> p a b", a=4))
            if h + 1 < 8:
                Wn = load_w(h + 1)

            def qk_exp(p, ka, m):
                r0, r1 = krows[m]
                qa = 2 * p
                q2 = slice(qa * 128, (qa + 2) * 128)
                sb_ = sbank()
                if ka < qa:
                    for kl in range(2):
                        kt = ka + kl
                        self.mm(sb_[:, kl * 256:(kl + 1) * 256], Kt[m][r0:r1, kt * 128:(kt + 1) * 128],
                                Q[m][r0:r1, q2], True, True)
                    ncol = 512
                    items = [(ka, 0, 0), (ka, 1, 128), (ka + 1, 0, 256), (ka + 1, 1, 384)]
                else:
                    self.mm(sb_[:, 0:256], Kt[m][r0:r1, qa * 128:(qa + 1) * 128], Q[m][r0:r1, q2], True, True,
                            skip=True)
                    self.mm(sb_[:, 0:128], self.idb[:, :], umask[:, :], False, True, skip=True)
                    qb_ = slice((qa + 1) * 128, (qa + 2) * 128)
                    self.mm(sb_[:, 256:384], Kt[m][r0:r1, qb_], Q[m][r0:r1, qb_], True, True, skip=True)
                    self.mm(sb_[:, 256:384], self.idb[:, :], umask[:, :], False, True, skip=True)
                    ncol = 384
                    items = [(qa, 0, 0), (qa, 1, 128), (qa + 1, 1, 256)]
                pb = Psb[self.rot("dP", NP_)]
                self.act(pb[:, 0:ncol], sb_[:, 0:ncol], AF.Exp)
                return (p, ka, m, items, pb)

            def pv(u):
                p, ka, m, items, pb = u
                for (kt, j, c0) in items:
                    qt = 2 * p + j
                    po = ps[2 + (qt % 4)]
                    self.mm(po[:, m * 256:m * 256 + 129], pb[:, c0:c0 + 128], V[:, kt, 0:129],
                            (kt == 0 and m == 0), kt == qt, skip=True)
                if m == 1 and ka == 2 * p:
                    epilogue(p)
                    if p > 0:
                        epi2(p - 1)

            def epi2(p):
                k2_ = p % 2
                for j in range(2):
                    self.tr(psT7[:, j * 128:(j + 1) * 128], nrmb[k2_][:, j, :], self.idb[:, :])
                self.cp("act", dst[h % 2][:, 2 * p * 128:(2 * p + 2) * 128], psT7[:, 0:256])

            def epilogue(p):
                k2_ = p % 2
                b0 = (2 * p) % 4
                pp = psbig[:, b0 * 512:(b0 + 2) * 512]
                pp4 = pp.rearrange("p (b m c) -> p b m c", b=2, m=2)
                r = rrb[k2_]
                r4 = r[:, 0:4].rearrange("p (b m) -> p b m", b=2)
                rl4 = r[:, 4:8].rearrange("p (b m) -> p b m", b=2)
                tmp, ob, sq, t2, nb_ = tmpb[k2_], osbb[k2_], sqb[k2_], t2b[k2_], nrmb[k2_]
                P.op("dve", lambda e: e.reciprocal(out=r4, in_=pp4[:, :, :, 128]), reads=[pp], writes=[r[:, 0:4]])
                self.tt("dve", rl4, r4, lamvec[:, :].unsqueeze(1).to_broadcast([128, 2, 2]), ALU.mult)
                P.op("dve", lambda e: e.tensor_tensor(out=tmp[:, :, :, :], in0=pp4[:, :, :, 0:128],
                                                      in1=rl4.unsqueeze(3).to_broadcast([128, 2, 2, 128]),
                                                      op=ALU.mult),
                     reads=[pp, r[:, 4:8]], writes=[tmp[:, :, :, :]])
                self.tt("dve", ob[:, :, :], tmp[:, :, 0, :], tmp[:, :, 1, :], ALU.add)
                self.tt("dve", sq[:, :, :], ob[:, :, :], ob[:, :, :], ALU.mult)
                P.op("dve", lambda e: e.reduce_sum(out=r[:, 8:10], in_=sq[:, :, :], axis=mybir.AxisListType.X),
                     reads=[sq[:, :, :]], writes=[r[:, 8:10]])
                self.ts("dve", r[:, 10:12], r[:, 8:10], 1.0 / 128, ALU.mult, EPS, ALU.add)
                self.tt("pool", r[:, 12:14], r[:, 10:12], self.nhalf[:, 0:2], ALU.pow)
                self.tt("dve", t2[:, :, :], ob[:, :, :], r[:, 12:14].unsqueeze(2).to_broadcast([128, 2, 128]),
                        ALU.mult)
                self.tt("dve", nb_[:, :, :], t2[:, :, :], sublnB[:, :].unsqueeze(1).to_broadcast([128, 2, 128]),
                        ALU.mult)

            pending = []
            for p in range(NPAIR):
                for ka in range(0, 2 * p + 2, 2):
                    for m in range(2):
                        pending.append(qk_exp(p, ka, m))
                        if len(pending) > LAG:
                            pv(pending.pop(0))
            while pending:
                pv(pending.pop(0))
            epi2(NPAIR - 1)
            self.dma("sp", diff_sp[:, :, h, :].rearrange("b p t -> p b t"),
                     dst[h % 2][:, :].rearrange("p (b t) -> p b t", b=8), f"dsp{h % 2}",
                     reads=[dst[h % 2][:, :]], dwrites=[("diff_sp", h)])

    def merge(self, win_d, wro_d, wdo_d, wout_d, ret_sp, diff_sp, tail=None):
        P, ps, hT, xT = self.P, self.ps, self.hT, self.xT
        A = self.arena
        o = [0]

        def al(name, shape, dtype):
            n = int(np.prod(shape[1:])) * ISZ[dtype]
            t = P.sb(name, shape, dtype, at=A + o[0])
            o[0] += (n + 63) // 64 * 64
            assert o[0] <= self.asz, ("merge arena", o[0], self.asz)
            return t
        TB = 256
        gin = [al(f"m_gin{k}", [128, 16, TB], BF16) for k in range(2)]
        din_ = [al(f"m_din{k}", [128, 8, TB], BF16) for k in range(2)]
        s0 = [al(f"m_s0{k}", [128, TB], F32) for k in range(2)]
        s1 = [al(f"m_s1{k}", [128, TB], F32) for k in range(2)]
        tq = [al(f"m_t{k}", [128, TB], F32) for k in range(2)]
        uq = [al(f"m_u{k}", [128, TB], F32) for k in range(2)]
        yT = [al(f"m_y{k}", [128, 2, TB], BF16) for k in range(2)]
        def load_q(qd):
            c0 = qd * 256
            a = self.wload("A", [(wro_d[:, c0:c0 + 256], 16, 256),
                                 (wdo_d[:, c0:c0 + 256], 8, 256),
                                 (win_d[:, O_G0 + c0:O_G0 + c0 + 256], 8, 256)])
            b = self.wload("B", [(win_d[:, O_G1 + c0:O_G1 + c0 + 256], 8, 256),
                                 (wout_d[c0:c0 + 256, :], 2, D)])
            return a + b

        def outproj(wou, tb, lastq=False):
            tsl = slice(tb * TB, (tb + 1) * TB)
            bi = tb % 2
            for dc in range(KC):
                po = ps[self.rot("mps", 8)]
                for yc in range(2):
                    self.mm(po[:, 0:TB], wou[:, yc, dc * 128:(dc + 1) * 128], yT[bi][:, yc, :], yc == 0, yc == 1)
                self.stt("dve", xT[:, dc, tsl], po[:, 0:TB], self.gatef[:, 8 + dc:8 + dc + 1], xT[:, dc, tsl],
                         ALU.mult, ALU.add)
            if lastq and tail is not None and tb % 2 == 1:
                tail(tb // 2)

        Wn = load_q(0)
        pend = None
        for qd in range(4):
            wro, wdo, wg0, wg1, wou = Wn
            if pend is not None:
                outproj(*pend)
                pend = None
            if qd + 1 < 4:
                Wn = load_q(qd + 1)
            for tb in range(S // TB):
                tsl = slice(tb * TB, (tb + 1) * TB)
                bi = tb % 2
                self.dma("sp", gin[bi][:, :, :], ret_sp[tb], f"mgi{bi}",
                         writes=[gin[bi][:, :, :]], dreads=[("ret_sp", hh, tb) for hh in range(4)])
                self.dma("sp", din_[bi][:, :, :], diff_sp[tb], f"mdi{bi}",
                         writes=[din_[bi][:, :, :]], dreads=[("diff_sp", hh) for hh in range(8)])
                for yc in range(2):
                    ys = slice(yc * 128, (yc + 1) * 128)
                    k = self.rot("mrot", 2)
                    pg0 = ps[self.rot("mps", 8)]
                    for kc in range(8):
                        self.mm(pg0[:, 0:TB], wg0[:, kc, ys], hT[:, kc, tsl], kc == 0, kc == 7)
                    pg1 = ps[self.rot("mps", 8)]
                    for kc in range(8):
                        self.mm(pg1[:, 0:TB], wg1[:, kc, ys], hT[:, kc, tsl], kc == 0, kc == 7)
                    self.act(s0[k][:, :], pg0[:, 0:TB], AF.Sigmoid)
                    self.act(s1[k][:, :], pg1[:, 0:TB], AF.Sigmoid)
                    pr = ps[self.rot("mps", 8)]
                    for kc in range(16):
                        self.mm(pr[:, 0:TB], wro[:, kc, ys], gin[bi][:, kc, :], kc == 0, kc == 15)
                    pd = ps[self.rot("mps", 8)]
                    for kc in range(8):
                        self.mm(pd[:, 0:TB], wdo[:, kc, ys], din_[bi][:, kc, :], kc == 0, kc == 7)
                    self.tt("dve", tq[k][:, :], pr[:, 0:TB], s0[k][:, :], ALU.mult)
                    self.tt("dve", uq[k][:, :], pd[:, 0:TB], s1[k][:, :], ALU.mult)
                    self.tt("dve", yT[bi][:, yc, :], tq[k][:, :], uq[k][:, :], ALU.add)
                if pend is not None:
                    outproj(*pend)
                pend = (wou, tb, qd == 3)
        outproj(*pend)

    def final_out(self, gfin):
        self.P.op("sp", lambda e: e.nop(), dreads=[f"out{t}" for t in range(NT)])

    def final_block(self, tb, r):
        ps, xT = self.ps, self.xT
        tsl = slice(tb * 512, (tb + 1) * 512)
        for dc in range(KC):
            self.stt("dve", self.yf[dc][:, :], xT[:, dc, tsl], self.gfin[:, dc:dc + 1], r[:, :], ALU.mult, ALU.mult)
        for c in range(4):
            t_ = tb * 4 + c
            ost = self.ost[t_ % 2]
            for half in range(2):
                bank = ps[self.rot("pso", 4)]
                for j in range(4):
                    dc = half * 4 + j
                    self.tr(bank[:, j * 128:(j + 1) * 128], self.yf[dc][:, c * 128:(c + 1) * 128], self.id32[:, :])
                self.cp("act" if half == 0 else "dve", ost[:, half * 512:(half + 1) * 512], bank[:, :])
            self.dma("sp", self.out_d[t_ * 128:(t_ + 1) * 128, :], ost[:, :], f"xout{t_ % 2}", reads=[ost[:, :]],
                     dwrites=[f"out{t_}"])


def _bf(a):
    return a.astype(ml_dtypes.bfloat16).astype(np.float64)


def host_consts():
    c = {}
    c["c_ident"] = np.eye(128, dtype=np.float32)
    pos = np.arange(128, dtype=np.float64)
    kdec = np.zeros((4, 128, 512), np.float32)
    xi = np.zeros((128, 4), np.float32)
    for h in range(4):
        log_g = np.log1p(-np.exp2(-5.0 - h))
        row = np.exp(-(pos + 1.0) * log_g) / 16.0
        kdec[h] = np.tile(row, 4)[None, :].astype(np.float32)
        xi[:, h] = np.exp((pos + 1.0) * log_g).astype(np.float32)
    c["c_kdec"] = kdec
    c["c_xi"] = xi
    r = np.arange(128)
    c["c_mask01"] = (r[None, :] >= r[:, None]).astype(np.float32)
    c["c_umask"] = np.where(r[None, :] < r[:, None], NEG, 0.0).astype(np.float32)
    p = np.arange(S, dtype=np.float64)
    qx = np.zeros((8, 6, S), np.float32)
    kx = np.zeros((8, 6, S), np.float32)
    for h in range(8):
        slope = 2.0 ** (-8.0 * (h + 1.0) / 8.0)
        a = slope * p
        a1 = _bf(a)
        a2 = _bf(a - a1)
        a3 = _bf(a - a1 - a2)
        qx[h, 0], qx[h, 1], qx[h, 2] = -a1, -a2, -a3
        qx[h, 3:6] = 1.0
        kx[h, 0:3] = 1.0
        kx[h, 3], kx[h, 4], kx[h, 5] = a1, a2, a3
    c["c_qx"] = qx.astype(ml_dtypes.bfloat16)
    c["c_kx"] = kx.astype(ml_dtypes.bfloat16)
    return c


def pvec(v, n):
    return np.ascontiguousarray(np.asarray(v, np.float32).reshape(n, 128).T)


def make_in_maps(inputs):
    f = lambda k: np.ascontiguousarray(np.asarray(inputs[k], np.float32))
    shared = dict(
        w_cond=f("w_cond")[0], b_cond=pvec(f("b_cond")[0], 72),
        g_norm=np.ascontiguousarray(np.concatenate([pvec(f("g_norm")[0, i], 8) for i in range(3)], axis=1)),
        w_ffn1_in=f("w_ffn1_in")[0], w_ffn1_out=f("w_ffn1_out")[0], w_in=f("w_in")[0],
        w_ret_out=f("w_ret_out")[0], diff_lambda=f("diff_lambda")[0].reshape(256),
        diff_subln=f("diff_subln")[0].reshape(128), w_diff_out=f("w_diff_out")[0], w_out=f("w_out")[0],
        w_ffn2_in=f("w_ffn2_in")[0], w_ffn2_out=f("w_ffn2_out")[0], g_final=pvec(f("g_final"), 8),
    )
    shared.update(host_consts())
    x = f("x")
    c = f("c")
    maps = []
    for b in range(8):
        m = dict(shared)
        m["x"] = np.ascontiguousarray(x[b])
        m["c"] = pvec(c[b], 8)
        maps.append(m)
    return maps


def build_nc(stage=None):
    nc = bass.Bass("TRN2", target_bir_lowering=False)
    k = K(nc, stage)
    P = k.build()
    return nc, P


def kernel(**inputs):
    nc, P = build_nc(None)
    maps = make_in_maps(inputs)
    res = run_bass_kernel_spmd(nc, maps, core_ids=list(range(8)))
    out = np.stack([np.asarray(r["out"], np.float32) for r in res.results], axis=0)
    return out
```

```python
import contextlib
import numpy as np
import ml_dtypes
import concourse.bass as bass
import concourse.mybir as mybir
from concourse.bass_utils import run_bass_kernel_spmd

dt = mybir.dt
AF = mybir.ActivationFunctionType
ALU = mybir.AluOpType
F32, BF16 = dt.float32, dt.bfloat16
ISZ = {dt.float32: 4, dt.bfloat16: 2, dt.int32: 4, dt.uint32: 4, dt.float16: 2}

D = 1024
S = 2048
DFF = 2816
KC = D // 128
NT = S // 128
NB = S // 512
EPS = 1e-6
O_RQ, O_RK, O_RV, O_RG, O_DQ, O_DK, O_DV, O_G0, O_G1 = 0, 1024, 2048, 4096, 6144, 7168, 8192, 9216, 10240
LAM_INIT = 0.8 - 0.6 * float(np.exp(-0.3 * 0))
NEG = -30000.0

ENGS = ("pe", "act", "dve", "pool", "sp")
SB_G = 128


class Prog:
    def __init__(self, nc):
        self.nc = nc
        self.ops = []
        self.sb_off = (int(nc.sbuf_base) + 63) // 64 * 64
        self.sb_top = int(nc.sbuf_top)
        self.tinfo = {}
        self.psum_banks = 0
        self.cells = {}

    def sb(self, name, shape, dtype, at=None):
        isz = ISZ[dtype]
        free = int(np.prod(shape[1:]))
        nbytes = free * isz
        if at is None:
            at = self.sb_off
            self.sb_off = (at + nbytes + 63) // 64 * 64
        assert at + nbytes <= self.sb_top, f"SBUF overflow at {name}: {at + nbytes} > {self.sb_top}"
        h = self.nc.alloc_sbuf_tensor_at(name, list(shape), dtype, offset=at)
        self.tinfo[h.name] = ("sb", at, free, isz)
        return h

    def ps(self, name):
        h = self.nc.alloc_psum_tensor(name, [128, 512], F32)
        self.tinfo[h.name] = ("ps", self.psum_banks, 512, 4)
        self.psum_banks += 1
        return h

    def cells_of(self, ap):
        name = ap.tensor.name
        info = self.tinfo.get(name)
        if info is None:
            return []
        space, base, free, isz0 = info
        if space == "ps":
            if free * isz0 <= 2048:
                return [("ps", base)]
            isz_ = ISZ[ap.dtype]
            pstr = free * isz0 // isz_
            e0_ = int(ap.offset) % pstr
            ext_ = 0
            for st, cnt in ap.ap[1:]:
                ext_ += (cnt - 1) * abs(st)
            lo_ = e0_ * isz_
            hi_ = (e0_ + ext_ + 1) * isz_
            return [("ps", base + b) for b in range(lo_ // 2048, (hi_ - 1) // 2048 + 1)]
        isz = ISZ[ap.dtype]
        pstride = free * isz0 // isz
        off = int(ap.offset)
        p0 = off // pstride
        e0 = off % pstride
        dims = ap.ap
        npart = dims[0][1]
        ext = 0
        for st, cnt in dims[1:]:
            ext += (cnt - 1) * abs(st)
        lo = base + e0 * isz
        hi = base + (e0 + ext + 1) * isz
        out = []
        for q in range(p0 // 32, (p0 + npart - 1) // 32 + 1):
            for c in range(lo // SB_G, (hi - 1) // SB_G + 1):
                out.append(("sb", q, c))
        return out

    def op(self, eng, fn, reads=(), writes=(), dma=None, dreads=(), dwrites=()):
        idx = len(self.ops)
        deps = set()
        rc, wc = [], []
        for a in reads:
            rc.extend(self.cells_of(a))
        for a in writes:
            wc.extend(self.cells_of(a))
        rc.extend(("dram", k) for k in dreads)
        wc.extend(("dram", k) for k in dwrites)
        ops = self.ops
        for c in rc:
            st = self.cells.get(c)
            if st is not None:
                if st[0] is not None:
                    deps.add(st[0])
                if c[0] == "ps":
                    for r in st[1].values():
                        if ops[r]["eng"] != eng:
                            deps.add(r)
        for c in wc:
            st = self.cells.get(c)
            if st is not None:
                if st[0] is not None:
                    deps.add(st[0])
                deps.update(st[1].values())
        rkey = ("d", idx) if dma is not None else eng
        for c in rc:
            st = self.cells.get(c)
            if st is None:
                st = [None, {}]
                self.cells[c] = st
            st[1][rkey] = idx
        for c in wc:
            st = self.cells.get(c)
            if st is None:
                st = [None, {}]
                self.cells[c] = st
            st[0] = idx
            st[1] = {}
        deps.discard(idx)
        self.ops.append(dict(eng=eng, fn=fn, deps=deps, dma=dma))
        return idx

    def emit(self):
        nc = self.nc
        ops = self.ops
        needs = [False] * len(ops)
        for o in ops:
            keep = set()
            latest = {}
            for d in o["deps"]:
                p = ops[d]
                if p["dma"] is not None:
                    keep.add(d)
                    continue
                if p["eng"] == "pe" and o["eng"] == "pe":
                    continue
                if d > latest.get(p["eng"], -1):
                    latest[p["eng"]] = d
            keep.update(latest.values())
            o["deps"] = keep
            for d in keep:
                needs[d] = True
        eng_cnt = {e: 0 for e in ENGS}
        dma_cnt = {}
        sig = [None] * len(ops)
        for i, o in enumerate(ops):
            if o["dma"] is not None:
                k = ("dma", o["dma"])
                dma_cnt[k] = dma_cnt.get(k, 0) + 16
                sig[i] = (k, dma_cnt[k])
            elif needs[i]:
                eng_cnt[o["eng"]] += 1
                sig[i] = (("eng", o["eng"]), eng_cnt[o["eng"]])
        semkeys = [("eng", e) for e in ENGS] + sorted(dma_cnt.keys())
        per_eng = {e: [] for e in ENGS}
        for i, o in enumerate(ops):
            per_eng[o["eng"]].append(i)
        self.stats = dict(n_ops=len(ops), per_eng={e: len(v) for e, v in per_eng.items()},
                          eng_cnt=dict(eng_cnt), n_sems=len(semkeys), waits={})
        with contextlib.ExitStack() as es:
            sems = {}
            for k in semkeys:
                sems[k] = es.enter_context(nc.semaphore("s_" + "_".join(str(x) for x in k)))
            block = es.enter_context(nc.Block())

            def run(engname, eng):
                waited = {}
                nwait = 0
                for i in per_eng[engname]:
                    o = ops[i]
                    req = {}
                    for d in o["deps"]:
                        k, v = sig[d]
                        if v > req.get(k, 0):
                            req[k] = v
                    for k, v in req.items():
                        if waited.get(k, 0) >= v:
                            continue
                        eng.wait_ge(sems[k], v)
                        waited[k] = v
                        nwait += 1
                    ins = o["fn"](eng)
                    if sig[i] is not None:
                        k, v = sig[i]
                        ins.then_inc(sems[k], 16 if k[0] == "dma" else 1)
                self.stats["waits"][engname] = nwait

            @block.tensor
            def _(e):
                run("pe", e)

            @block.scalar
            def _(e):
                run("act", e)

            @block.vector
            def _(e):
                run("dve", e)

            @block.gpsimd
            def _(e):
                run("pool", e)

            @block.sync
            def _(e):
                run("sp", e)


class K:
    def __init__(self, nc, stage=None):
        self.nc = nc
        self.stage = stage
        self.P = Prog(nc)
        self.rr = {}

    def mm(self, out, lhsT, rhs, start, stop, skip=False):
        if skip:
            self.P.op("pe", lambda e: e.matmul(out, lhsT=lhsT, rhs=rhs, start=start, stop=stop,
                                               skip_group_check=True),
                      reads=[lhsT, rhs], writes=[out])
        else:
            self.P.op("pe", lambda e: e.matmul(out, lhsT=lhsT, rhs=rhs, start=start, stop=stop),
                      reads=[lhsT, rhs], writes=[out])

    def tr(self, out, in_, ident):
        self.P.op("pe", lambda e: e.transpose(out=out, in_=in_, identity=ident),
                  reads=[in_, ident], writes=[out])

    def act(self, out, in_, func, scale=1.0, bias=None, accum=None):
        reads = [in_]
        kw = {}
        if not isinstance(scale, (int, float)):
            reads.append(scale)
        if bias is not None:
            kw["bias"] = bias
            if not isinstance(bias, (int, float)):
                reads.append(bias)
        writes = [out]
        if accum is not None:
            kw["accum_out"] = accum
            writes.append(accum)
        self.P.op("act", lambda e: e.activation(out=out, in_=in_, func=func, scale=scale, **kw),
                  reads=reads, writes=writes)

    def tt(self, eng, out, in0, in1, op):
        self.P.op(eng, lambda e: e.tensor_tensor(out=out, in0=in0, in1=in1, op=op),
                  reads=[in0, in1], writes=[out])

    def ts(self, eng, out, in0, s1, op0, s2=None, op1=None):
        reads = [in0] + [s for s in (s1, s2) if s is not None and not isinstance(s, (int, float))]
        if op1 is None:
            self.P.op(eng, lambda e: e.tensor_scalar(out=out, in0=in0, scalar1=s1, scalar2=None, op0=op0),
                      reads=reads, writes=[out])
        else:
            self.P.op(eng, lambda e: e.tensor_scalar(out=out, in0=in0, scalar1=s1, scalar2=s2, op0=op0, op1=op1),
                      reads=reads, writes=[out])

    def stt(self, eng, out, in0, scalar, in1, op0, op1):
        reads = [in0, in1] + ([] if isinstance(scalar, (int, float)) else [scalar])
        self.P.op(eng, lambda e: e.scalar_tensor_tensor(out=out, in0=in0, scalar=scalar, in1=in1, op0=op0, op1=op1),
                  reads=reads, writes=[out])

    def cp(self, eng, out, in_):
        if eng == "act":
            self.act(out, in_, AF.Copy)
        else:
            self.P.op(eng, lambda e: e.tensor_copy(out=out, in_=in_), reads=[in_], writes=[out])

    def memset(self, eng, ap, val):
        self.P.op(eng, lambda e: e.memset(ap, val), writes=[ap])

    def dma(self, q, out, in_, sem, reads=(), writes=(), dreads=(), dwrites=()):
        self.P.op(q, lambda e: e.dma_start(out=out, in_=in_), reads=list(reads), writes=list(writes),
                  dma=sem, dreads=dreads, dwrites=dwrites)

    def rot(self, key, n):
        v = self.rr.get(key, 0)
        self.rr[key] = v + 1
        return v % n

    def wload(self, ring, parts, after=()):
        slots = self.wA if ring == "A" else self.wB
        i = self.rot("w" + ring, len(slots))
        slot = slots[i]
        off = 0
        views = []
        for pi, (src, kc, ncols) in enumerate(parts):
            n = kc * ncols
            v = slot[:, off:off + n].rearrange("p (k c) -> p k c", k=kc)
            self.dma("pool", v, src.rearrange("(k p) c -> p k c", p=128), f"w{ring}{i}_{pi}", writes=[v],
                     dreads=after)
            views.append(v)
            off += n
        assert off <= slot.shape[1]
        return views

    def build(self):
        nc, P = self.nc, self.P
        stage = self.stage
        din = lambda name, shape: nc.dram_tensor(name, list(shape), F32, kind="ExternalInput").ap()
        x_d = din("x", [S, D])
        c_d = din("c", [128, KC])
        wcond_d = din("w_cond", [D, 9 * D])
        bcond_d = din("b_cond", [128, 72])
        gnorm_d = din("g_norm", [128, 24])
        wf_in = [din("w_ffn1_in", [D, 2 * DFF]), din("w_ffn2_in", [D, 2 * DFF])]
        wf_out = [din("w_ffn1_out", [DFF, D]), din("w_ffn2_out", [DFF, D])]
        win_d = din("w_in", [D, 11264])
        wro_d = din("w_ret_out", [2048, D])
        lam_d = din("diff_lambda", [256])
        subln_d = din("diff_subln", [128])
        wdo_d = din("w_diff_out", [D, D])
        wout_d = din("w_out", [D, D])
        gfin_d = din("g_final", [128, KC])
        ident_d = din("c_ident", [128, 128])
        kdec_d = din("c_kdec", [4, 128, 512])
        xi_d = din("c_xi", [128, 4])
        m01_d = din("c_mask01", [128, 128])
        um_d = din("c_umask", [128, 128])
        qx_d = nc.dram_tensor("c_qx", [8, 6, S], BF16, kind="ExternalInput").ap()
        kx_d = nc.dram_tensor("c_kx", [8, 6, S], BF16, kind="ExternalInput").ap()
        out_d = nc.dram_tensor("out", [S, D], F32, kind="ExternalOutput").ap()
        spk = "ExternalOutput" if stage in ("ret", "diff") else "Internal"
        ret_sp = nc.dram_tensor("ret_sp", [8, 128, 16, 256], BF16, kind=spk).ap()
        diff_sp = nc.dram_tensor("diff_sp", [8, 128, 8, 256], BF16, kind=spk).ap()
        if stage is not None:
            dbg_x = nc.dram_tensor("dbg_x", [128, KC, S], F32, kind="ExternalOutput").ap()
            dbg_h = nc.dram_tensor("dbg_h", [128, KC, S], BF16, kind="ExternalOutput").ap()
            dbg_m = nc.dram_tensor("dbg_m", [128, 80], F32, kind="ExternalOutput").ap()
        self.x_d, self.out_d = x_d, out_d

        self.xT = xT = P.sb("xT", [128, KC, S], F32)
        self.hT = hT = P.sb("hT", [128, KC, S], BF16)
        self.wA = [P.sb(f"wA{i}", [128, 8192], BF16) for i in range(2)]
        self.wB = [P.sb(f"wB{i}", [128, 4096], BF16) for i in range(2)]
        id32 = P.sb("id32", [128, 128], F32)
        idb = P.sb("idb", [128, 128], BF16)
        ones32 = P.sb("ones32", [128, 128], F32)
        nhalf = P.sb("nhalf", [128, 512], F32)
        m01 = P.sb("m01", [128, 128], F32)
        umask = P.sb("umask", [128, 128], BF16)
        sublnB = P.sb("sublnB", [128, 128], F32)
        kdec = P.sb("kdec", [128, 4, 512], F32)
        xi = P.sb("xi", [128, 4], F32)
        c_sb = P.sb("c_sb", [128, KC], F32)
        c_bf = P.sb("c_bf", [128, KC], BF16)
        self._c_bf = c_bf
        bcond = P.sb("bcond", [128, 72], F32)
        gnorm = P.sb("gnorm", [128, 24], F32)
        gfin = P.sb("gfin", [128, KC], F32)
        modT = P.sb("modT", [128, 72], F32)
        gs = P.sb("gs", [128, 24], F32)
        gatef = P.sb("gatef", [128, 24], F32)
        lamb = P.sb("lamb", [128, 256], F32)
        lsm = P.sb("lsm", [128, 72], F32)
        neglam = P.sb("neglam", [128, 1], F32)
        self.id32, self.idb, self.ones32, self.nhalf = id32, idb, ones32, nhalf
        self.gs, self.gatef, self.modT = gs, gatef, modT
        arena = P.sb_off
        self.arena = arena
        asz = P.sb_top - arena
        self.asz = asz
        ps_lo = [P.ps(f"ps{i}") for i in range(2)]
        self.psbig = nc.alloc_psum_tensor("psbig", [128, 2048], F32)
        P.tinfo[self.psbig.name] = ("ps", 2, 2048, 4)
        P.psum_banks += 4
        ps_hi = [P.ps(f"ps{i}") for i in (6, 7)]
        self.ps = ps_lo + [self.psbig[:, 512 * j:512 * (j + 1)] for j in range(4)] + ps_hi
        ps = self.ps

        sp = "sp"
        self.dma(sp, id32[:, :], ident_d, "c0", writes=[id32[:, :]])
        self.dma("pool", idb[:, :], ident_d, "c1", writes=[idb[:, :]])
        self.dma(sp, m01[:, :], m01_d, "c2", writes=[m01[:, :]])
        self.dma("pool", umask[:, :], um_d, "c3", writes=[umask[:, :]])
        self.dma(sp, kdec[:, :, :], kdec_d.rearrange("h p t -> p h t"), "c4", writes=[kdec[:, :, :]])
        self.dma(sp, xi[:, :], xi_d, "c5", writes=[xi[:, :]])
        self.dma(sp, c_sb[:, :], c_d, "c6", writes=[c_sb[:, :]])
        self.dma(sp, bcond[:, :], bcond_d, "c7", writes=[bcond[:, :]])
        self.dma(sp, gnorm[:, :], gnorm_d, "c8", writes=[gnorm[:, :]])
        self.dma(sp, gfin[:, :], gfin_d, "c9", writes=[gfin[:, :]])
        self.dma(sp, lamb[:, :], lam_d.partition_broadcast(128), "c10", writes=[lamb[:, :]])
        self.dma(sp, sublnB[:, :], subln_d.partition_broadcast(128), "c11", writes=[sublnB[:, :]])
        self.memset("dve", ones32[:, :], 1.0)
        self.memset("dve", nhalf[:, :], -0.5)
        self.ts("dve", sublnB[:, :], sublnB[:, :], 1.0 - LAM_INIT, ALU.mult)
        self.act(c_bf[:, :], c_sb[:, :], AF.Silu)
        self.tt("dve", lsm[:, 0:64], lamb[:, 0:64], lamb[:, 64:128], ALU.mult)
        P.op("dve", lambda e: e.reduce_sum(out=lsm[:, 64:65], in_=lsm[:, 0:64], axis=mybir.AxisListType.X),
             reads=[lsm[:, 0:64]], writes=[lsm[:, 64:65]])
        self.tt("dve", lsm[:, 0:64], lamb[:, 128:192], lamb[:, 192:256], ALU.mult)
        P.op("dve", lambda e: e.reduce_sum(out=lsm[:, 65:66], in_=lsm[:, 0:64], axis=mybir.AxisListType.X),
             reads=[lsm[:, 0:64]], writes=[lsm[:, 65:66]])
        self.act(lsm[:, 66:68], lsm[:, 64:66], AF.Exp)
        self.tt("dve", lsm[:, 68:69], lsm[:, 67:68], lsm[:, 66:67], ALU.subtract)
        self.ts("dve", neglam[:, :], lsm[:, 68:69], -LAM_INIT, ALU.add)

        xs = [P.sb(f"xs{i}", [128, D], F32, at=arena + i * 4096) for i in range(4)]
        for tt_ in range(NT):
            b = tt_ % 4
            self.dma(sp, xs[b][:, :], x_d[tt_ * 128:(tt_ + 1) * 128, :], f"xin{b}", writes=[xs[b][:, :]],
                     dwrites=[("xk", tt_)])
            for half in range(2):
                bank = ps[self.rot("psx", 4)]
                for j in range(4):
                    dc = half * 4 + j
                    self.tr(bank[:, j * 128:(j + 1) * 128], xs[b][:, dc * 128:(dc + 1) * 128], id32[:, :])
                dst = xT[:, half * 4:half * 4 + 4, tt_ * 128:(tt_ + 1) * 128]
                src = bank[:, :].rearrange("p (a b) -> p a b", a=4)
                self.cp("act" if half == 0 else "dve", dst, src)

        self.wC = P.sb("wC", [128, 4096], BF16, at=arena + 34816)
        self.norm_stats(0)
        self.mod_groups(wcond_d, bcond, gnorm, 0)
        self.norm_apply(0)
        if stage == "norm0":
            return self.finish_debug(dbg_x, dbg_h, dbg_m)
        halves = [(g, hf) for g in range(3, 9) for hf in range(2)]

        def extra(gi):
            for (g, hf) in halves[2 * gi:2 * gi + 2]:
                self.mod_half(wcond_d, g, hf)
        def extra2(gi):
            extra(gi)
            if gi == 5:
                self.mod_finish(bcond, gnorm, 1)
                self.mod_finish(bcond, gnorm, 2)
        self.ffn(0, wf_in[0], wf_out[0], extra=extra2, tail=lambda tb: self.norm_block(1, tb))
        if stage == "ffn1":
            return self.finish_debug(dbg_x, dbg_h, dbg_m)
        self._mw = (win_d, wro_d, wdo_d, wout_d)
        self.retention(win_d, kdec, xi, m01, ret_sp)
        if stage == "ret":
            return self.finish_debug(dbg_x, dbg_h, dbg_m)
        self.diffattn(win_d, qx_d, kx_d, umask, sublnB, neglam, diff_sp)
        if stage == "diff":
            return self.finish_debug(dbg_x, dbg_h, dbg_m)
        self.merge(win_d, wro_d, wdo_d, wout_d, ret_sp, diff_sp, tail=lambda tb: self.norm_block(2, tb))
        if stage == "mix":
            return self.finish_debug(dbg_x, dbg_h, dbg_m)
        self.gfin = gfin
        self.yf = [P.sb(f"yf{k}", [128, 512], F32, at=arena + k * 2048) for k in range(8)]
        self.ost = [P.sb(f"ost{k}", [128, D], F32, at=arena + 34816 + 6144 + k * 4096) for k in range(2)]
        self.ffn(2, wf_in[1], wf_out[1], tail=lambda tb: self.norm_block(3, tb))
        self.final_out(gfin)
        P.emit()
        return P

    def finish_debug(self, dbg_x, dbg_h, dbg_m):
        xT, hT = self.xT, self.hT
        self.dma("sp", dbg_x, xT[:, :, :], "dbg0", reads=[xT[:, :, :]], dwrites=["dbgx"])
        self.dma("sp", dbg_h, hT[:, :, :], "dbg1", reads=[hT[:, :, :]], dwrites=["dbgh"])
        self.dma("sp", dbg_m[:, 0:72], self.modT[:, :], "dbg2", reads=[self.modT[:, :]], dwrites=["dbgm"])
        keys = ["dbgx", "dbgh", "dbgm"]
        if self.stage in ("ret", "diff"):
            keys += [("ret_sp", hh, tb) for hh in range(4) for tb in range(8)]
        if self.stage == "diff":
            keys += [("diff_sp", hh) for hh in range(8)]
        self.P.op("sp", lambda e: e.nop(), dreads=keys)
        self.P.emit()
        return self.P

    def mod_groups(self, wcond_d, bcond, gnorm, i):
        ps, modT, gs, gatef = self.ps, self.modT, self.gs, self.gatef
        bank = ps[7]
        for j in range(3):
            g = i * 3 + j
            (w,) = self.wload("A", [(wcond_d[:, g * D:(g + 1) * D], KC, D)],
                              after=[("xk", 9)] if (i == 0 and j == 0) else ())
            for fc in range(KC):
                col = g * 8 + fc
                for kc in range(KC):
                    self.mm(bank[:, col:col + 1], w[:, kc, fc * 128:(fc + 1) * 128], self.c_bf_ap(kc),
                            kc == 0, kc == KC - 1)
        c0 = i * 24
        self.tt("dve", modT[:, c0:c0 + 24], bank[:, c0:c0 + 24], bcond[:, c0:c0 + 24], ALU.add)
        self.stt("dve", gs[:, i * 8:i * 8 + 8], modT[:, c0 + 8:c0 + 16], 1.0, gnorm[:, i * 8:i * 8 + 8],
                 ALU.add, ALU.mult)
        self.ts("dve", gatef[:, i * 8:i * 8 + 8], modT[:, c0 + 16:c0 + 24], 1.0 if i == 1 else 0.5, ALU.mult)

    def mod_half(self, wcond_d, g, half):
        P, ps = self.P, self.ps
        slot = self.wC
        v = slot[:, :].rearrange("p (k c) -> p k c", k=KC)
        src = wcond_d[:, g * D + half * 512:g * D + (half + 1) * 512]
        self.dma("pool", v, src.rearrange("(k p) c -> p k c", p=128), "wC", writes=[v])
        bank = ps[7]
        for f4 in range(4):
            col = g * 8 + half * 4 + f4
            for kc in range(KC):
                self.mm(bank[:, col:col + 1], v[:, kc, f4 * 128:(f4 + 1) * 128], self.c_bf_ap(kc),
                        kc == 0, kc == KC - 1)

    def mod_finish(self, bcond, gnorm, i):
        ps, modT, gs, gatef = self.ps, self.modT, self.gs, self.gatef
        bank = ps[7]
        c0 = i * 24
        self.tt("dve", modT[:, c0:c0 + 24], bank[:, c0:c0 + 24], bcond[:, c0:c0 + 24], ALU.add)
        self.stt("dve", gs[:, i * 8:i * 8 + 8], modT[:, c0 + 8:c0 + 16], 1.0, gnorm[:, i * 8:i * 8 + 8],
                 ALU.add, ALU.mult)
        self.ts("dve", gatef[:, i * 8:i * 8 + 8], modT[:, c0 + 16:c0 + 24], 1.0 if i == 1 else 0.5, ALU.mult)

    def c_bf_ap(self, kc):
        return self._c_bf[:, kc:kc + 1]

    def norm_modulate(self, i):
        self.norm_stats(i)
        self.norm_apply(i)

    def norm_bufs(self, i):
        P, A = self.P, self.arena
        sq = [P.sb(f"sq{i}_{k}", [128, 512], F32, at=A + k * 2048) for k in range(2)]
        vt = [P.sb(f"vt{i}_{k}", [128, 512], F32, at=A + 4096 + k * 2048) for k in range(2)]
        rstd = [P.sb(f"rstd{i}_{k}", [128, 512], F32, at=A + 8192 + k * 2048) for k in range(4)]
        return sq, vt, rstd

    def norm_stats(self, i):
        ps, xT = self.ps, self.xT
        sq, vt, rstd = self.norm_bufs(i)
        self._rstd = rstd
        for tb in range(NB):
            tsl = slice(tb * 512, (tb + 1) * 512)
            bank = ps[4 + self.rot("psn", 2)]
            for dc in range(KC):
                s_ = sq[self.rot("sq", 2)]
                self.act(s_[:, :], xT[:, dc, tsl], AF.Square)
                self.mm(bank[:, :], self.ones32[:, :], s_[:, :], dc == 0, dc == KC - 1)
            v = vt[tb % 2]
            self.ts("dve", v[:, :], bank[:, :], 1.0 / D, ALU.mult, EPS, ALU.add)
            self.act(v[:, :], v[:, :], AF.Sqrt)
            r = rstd[tb]
            self.P.op("dve", lambda e, r=r, v=v: e.reciprocal(out=r[:, :], in_=v[:, :]),
                      reads=[v[:, :]], writes=[r[:, :]])

    def norm_block(self, i, tb):
        P, ps, xT, hT = self.P, self.ps, self.xT, self.hT
        B0 = self.arena + 34816
        sq = [P.sb(f"nbsq{i}_{tb}_{k}", [128, 512], F32, at=B0 + k * 2048) for k in range(2)]
        if i == 3:
            v = P.sb(f"nbr{i}_{tb}", [128, 512], F32, at=B0 + 4096)
        else:
            v = P.sb(f"nbv{i}_{tb}", [128, 512], F32, at=B0 + 4096)
            tmp = [P.sb(f"nbt{i}_{tb}_{k}", [128, 512], F32, at=B0 + 6144 + k * 2048) for k in range(2)]
        tsl = slice(tb * 512, (tb + 1) * 512)
        bank = ps[6 + self.rot("psnb", 2)]
        for dc in range(KC):
            s_ = sq[self.rot("sq", 2)]
            self.act(s_[:, :], xT[:, dc, tsl], AF.Square)
            self.mm(bank[:, :], self.ones32[:, :], s_[:, :], dc == 0, dc == KC - 1)
        self.ts("dve", v[:, :], bank[:, :], 1.0 / D, ALU.mult, EPS, ALU.add)
        self.act(v[:, :], v[:, :], AF.Sqrt)
        self.P.op("dve", lambda e: e.reciprocal(out=v[:, :], in_=v[:, :]), reads=[v[:, :]], writes=[v[:, :]])
        if i == 3:
            self.final_block(tb, v)
            return
        for dc in range(KC):
            t = tmp[self.rot("ntmp", 2)]
            self.stt("dve", t[:, :], xT[:, dc, tsl], self.gs[:, i * 8 + dc:i * 8 + dc + 1], v[:, :],
                     ALU.mult, ALU.mult)
            self.act(hT[:, dc, tsl], t[:, :], AF.Identity, bias=self.modT[:, i * 24 + dc:i * 24 + dc + 1])

    def norm_apply(self, i):
        P, A, xT, hT = self.P, self.arena, self.xT, self.hT
        rstd = self._rstd
        tmp = [P.sb(f"ntmp{i}_{k}", [128, 512], F32, at=A + 16384 + k * 2048) for k in range(2)]
        for tb in range(NB):
            tsl = slice(tb * 512, (tb + 1) * 512)
            r = rstd[tb]
            if i == 3:
                self.final_block(tb, r)
                continue
            for dc in range(KC):
                t = tmp[self.rot("ntmp", 2)]
                self.stt("dve", t[:, :], xT[:, dc, tsl], self.gs[:, i * 8 + dc:i * 8 + dc + 1], r[:, :],
                         ALU.mult, ALU.mult)
                self.act(hT[:, dc, tsl], t[:, :], AF.Identity, bias=self.modT[:, i * 24 + dc:i * 24 + dc + 1])

    def ffn(self, i, w_in_d, w_out_d, extra=None, tail=None):
        P, ps, xT, hT = self.P, self.ps, self.xT, self.hT
        A = self.arena
        actb = [P.sb(f"actb{i}_{k}", [128, 4, S], BF16, at=A + k * 16384) for k in range(2)]
        sa = [P.sb(f"sa{i}_{k}", [128, 512], BF16, at=A + 32768 + k * 1024) for k in range(2)]
        groups = [(g * 512, 512) for g in range(5)] + [(2560, 256)]
        for gi, (f0, fw) in enumerate(groups):
            nf = fw // 128
            wa, wb = self.wload("A", [(w_in_d[:, f0:f0 + fw], KC, fw),
                                      (w_in_d[:, DFF + f0:DFF + f0 + fw], KC, fw)])
            (wo,) = self.wload("B", [(w_out_d[f0:f0 + fw, :], nf, D)])
            ab = actb[gi % 2]
            for tb in range(NB):
                tsl = slice(tb * 512, (tb + 1) * 512)
                for ft in range(nf):
                    pa = ps[self.rot("ffa", 2)]
                    pb = ps[2 + self.rot("ffb", 2)]
                    for kc in range(KC):
                        self.mm(pa[:, :], wa[:, kc, ft * 128:(ft + 1) * 128], hT[:, kc, tsl], kc == 0, kc == KC - 1)
                    for kc in range(KC):
                        self.mm(pb[:, :], wb[:, kc, ft * 128:(ft + 1) * 128], hT[:, kc, tsl], kc == 0, kc == KC - 1)
                    s = sa[self.rot("sa", 2)]
                    self.act(s[:, :], pa[:, :], AF.Silu)
                    self.tt("dve", ab[:, ft, tsl], pb[:, :], s[:, :], ALU.mult)
            if extra is not None:
                extra(gi)
            for tb in range(NB):
                tsl = slice(tb * 512, (tb + 1) * 512)
                for dc in range(KC):
                    po = ps[4 + self.rot("ffo", 2)]
                    for ft in range(nf):
                        self.mm(po[:, :], wo[:, ft, dc * 128:(dc + 1) * 128], ab[:, ft, tsl], ft == 0, ft == nf - 1)
                    self.stt("dve", xT[:, dc, tsl], po[:, :], self.gatef[:, i * 8 + dc:i * 8 + dc + 1],
                             xT[:, dc, tsl], ALU.mult, ALU.add)
                if tail is not None and gi == len(groups) - 1 and tb > 0:
                    tail(tb - 1)
        if tail is not None:
            tail(NB - 1)

    def diff_load_w(self, win_d, hh):
        return self.wload("B", [(win_d[:, O_DQ + hh * 128:O_DQ + (hh + 1) * 128], KC, 128),
                                (win_d[:, O_DK + hh * 128:O_DK + (hh + 1) * 128], KC, 128),
                                (win_d[:, O_DV + hh * 128:O_DV + (hh + 1) * 128], KC, 128)])

    def retention(self, win_d, kdec, xi, m01, ret_sp):
        P, ps, hT = self.P, self.ps, self.hT
        A = self.arena
        o = [0]

        def al(name, shape, dtype):
            n = int(np.prod(shape[1:])) * ISZ[dtype]
            t = P.sb(name, shape, dtype, at=A + o[0])
            o[0] += (n + 63) // 64 * 64
            assert o[0] <= self.asz, ("ret arena", o[0], self.asz)
            return t
        qT = [al(f"r_qT{k}", [128, 2, 512], BF16) for k in range(2)]
        kT = [al(f"r_kT{k}", [128, 2, 512], BF16) for k in range(2)]
        vb = [al(f"r_v{k}", [128, 4, 512], BF16) for k in range(2)]
        sg = [al(f"r_sg{k}", [128, 4, 512], BF16) for k in range(2)]
        R32 = al("r_R32", [128, 2, 512], F32)
        Rbf = al("r_Rbf", [128, 2, 512], BF16)
        k2 = [al(f"r_k2{k}", [128, 256], BF16) for k in range(2)]
        PT = [al(f"r_PT{k}", [128, 128], BF16) for k in range(2)]
        osb = [al(f"r_osb{k}", [128, 512], F32) for k in range(2)]
        nbf = [al(f"r_nbf{k}", [128, 512], BF16) for k in range(2)]
        gtd = [al(f"r_gtd{k}", [128, 512], BF16) for k in range(2)]
        gst = [al(f"r_gst{k}", [128, 4, 512], BF16) for k in range(2)]
        st6 = [al(f"r_st{k}", [128, 6], F32) for k in range(2)]
        mv = [al(f"r_mv{k}", [128, 8], F32) for k in range(2)]
        psT6 = ps[6][:, :].bitcast(BF16)
        psT7 = ps[7][:, :].bitcast(BF16)
        W = {}

        def load_head(h):
            wq, wk, wv = self.wload("A", [(win_d[:, O_RQ + h * 256:O_RQ + (h + 1) * 256], KC, 256),
                                          (win_d[:, O_RK + h * 256:O_RK + (h + 1) * 256], KC, 256),
                                          (win_d[:, O_RV + h * 512:O_RV + (h + 1) * 512], KC, 512)])
            (wg,) = self.wload("B", [(win_d[:, O_RG + h * 512:O_RG + (h + 1) * 512], KC, 512)])
            W[h] = (wq, wk, wv, wg)

        def proj_piece(bidx, c):
            h, tb = divmod(bidx, 4)
            wq, wk, wv, wg = W[h]
            bi = bidx % 2
            tsl = slice(tb * 512, (tb + 1) * 512)
            bank = ps[self.rot("rpj", 2)]
            if c < 2:
                for kc in range(KC):
                    self.mm(bank[:, :], wq[:, kc, c * 128:(c + 1) * 128], hT[:, kc, tsl], kc == 0, kc == KC - 1)
                self.cp("act", qT[bi][:, c, :], bank[:, :])
            else:
                dcq = c - 2
                for kc in range(KC):
                    self.mm(bank[:, :], wk[:, kc, dcq * 128:(dcq + 1) * 128], hT[:, kc, tsl], kc == 0, kc == KC - 1)
                self.tt("dve", kT[bi][:, dcq, :], bank[:, :], kdec[:, h, :], ALU.mult)
            tok = slice(tb * 512 + c * 128, tb * 512 + (c + 1) * 128)
            bank = ps[self.rot("rpj", 2)]
            for kc in range(KC):
                self.mm(bank[:, :], hT[:, kc, tok], wv[:, kc, :], kc == 0, kc == KC - 1)
            self.cp("act", vb[bi][:, c, :], bank[:, :])
            bank = ps[self.rot("rpj", 2)]
            for kc in range(KC):
                self.mm(bank[:, :], hT[:, kc, tok], wg[:, kc, :], kc == 0, kc == KC - 1)
            self.act(sg[bi][:, c, :], bank[:, :], AF.Silu)

        def tg(bidx, c):
            h, tb = divmod(bidx, 4)
            bi = bidx % 2
            j = (bidx * 4 + c) % 2
            cs = slice(c * 128, (c + 1) * 128)
            for e4 in range(4):
                self.tr(psT7[:, e4 * 128:(e4 + 1) * 128], gtd[j][:, e4 * 128:(e4 + 1) * 128], self.idb[:, :])
            self.cp("act", gst[bi][:, :, cs], psT7[:, 0:512].rearrange("p (a b) -> p a b", a=4))
            if c == 3:
                for half in range(2):
                    src = gst[bi][:, :, half * 256:(half + 1) * 256]
                    self.dma("sp", ret_sp[2 * tb + half, :, h * 4:(h + 1) * 4, :], src, f"rsp{bi}{half}",
                             reads=[src], dwrites=[("ret_sp", h, 2 * tb + half)])

        def e2(bidx, c):
            bi = bidx % 2
            j = (bidx * 4 + c) % 2
            m = mv[j]
            self.act(nbf[j][:, :], osb[j][:, :], AF.Identity, scale=m[:, 3:4], bias=m[:, 4:5])
            self.tt("dve", gtd[j][:, :], nbf[j][:, :], sg[bi][:, c, :], ALU.mult)

        load_head(0)
        for c in range(4):
            proj_piece(0, c)
        pend1 = None
        pend2 = None
        for bidx in range(16):
            h, tb = divmod(bidx, 4)
            bi = bidx % 2
            gam = 1.0 - 2.0 ** (-5 - h)
            gC = float(gam ** 128)
            if tb == 0:
                if h + 1 < 4:
                    load_head(h + 1)
                self.memset("pool", R32[:, :, :], 0.0)
            if h == 3 and tb == 1:
                self._diffW0 = self.diff_load_w(win_d, 0)
            for c in range(4):
                n = tb * 4 + c
                cs = slice(c * 128, (c + 1) * 128)
                j = (bidx * 4 + c) % 2
                last = (n == NT - 1)
                for dcq in range(2):
                    self.mm(ps[2][:, 0:128], kT[bi][:, dcq, cs], qT[bi][:, dcq, cs], dcq == 0, dcq == 1)
                self.tt("dve", PT[j][:, :], ps[2][:, 0:128], m01[:, :], ALU.mult)
                if not last:
                    for dcq in range(2):
                        self.tr(psT6[:, dcq * 128:(dcq + 1) * 128], kT[bi][:, dcq, cs], self.idb[:, :])
                    self.act(k2[j][:, :], psT6[:, 0:256], AF.Copy, scale=gC)
                if bidx + 1 < 16:
                    proj_piece(bidx + 1, c)
                new_e2 = None
                if pend1 is not None:
                    e2(*pend1)
                    new_e2 = pend1
                    pend1 = None
                self.mm(ps[3][:, :], PT[j][:, :], vb[bi][:, c, :], True, n == 0)
                if n > 0:
                    for dcq in range(2):
                        self.mm(ps[3][:, :], qT[bi][:, dcq, cs], Rbf[:, dcq, :], False, dcq == 1)
                if not last:
                    for dcq in range(2):
                        self.mm(ps[4 + dcq][:, :], k2[j][:, dcq * 128:(dcq + 1) * 128], vb[bi][:, c, :], True, True)
                    for dcq in range(2):
                        self.stt("dve", R32[:, dcq, :], R32[:, dcq, :], gC, ps[4 + dcq][:, :], ALU.mult, ALU.add)
                        self.cp("dve", Rbf[:, dcq, :], R32[:, dcq, :])
                if pend2 is not None:
                    tg(*pend2)
                    pend2 = None
                self.act(osb[j][:, :], ps[3][:, :], AF.Identity, scale=xi[:, h:h + 1])
                s6, m = st6[j], mv[j]
                P.op("dve", lambda e, s6=s6, ob=osb[j]: e.bn_stats(out=s6[:, :], in_=ob[:, :]),
                     reads=[osb[j][:, :]], writes=[s6[:, :]])
                P.op("dve", lambda e, s6=s6, m=m: e.bn_aggr(out=m[:, 0:2], in_=s6[:, :]),
                     reads=[s6[:, :]], writes=[m[:, 0:2]])
                self.ts("dve", m[:, 2:3], m[:, 1:2], EPS, ALU.add)
                self.tt("pool", m[:, 3:4], m[:, 2:3], self.nhalf[:, 0:1], ALU.pow)
                self.stt("dve", m[:, 4:5], m[:, 0:1], -1.0, m[:, 3:4], ALU.mult, ALU.mult)
                pend1 = (bidx, c)
                pend2 = new_e2
        e2(*pend1)
        tg(*pend2)
        tg(*pend1)

    def diffattn(self, win_d, qx_d, kx_d, umask, sublnB, neglam, diff_sp):
        P, ps, hT = self.P, self.ps, self.hT
        A = self.arena
        o = [0]

        def al(name, shape, dtype):
            n = int(np.prod(shape[1:])) * ISZ[dtype]
            t = P.sb(name, shape, dtype, at=A + o[0])
            o[0] += (n + 63) // 64 * 64
            assert o[0] <= self.asz, ("diff arena", o[0], self.asz)
            return t
        Q = [al(f"d_Q{m}", [128, S], BF16) for m in range(2)]
        Kt = [al(f"d_K{m}", [128, S], BF16) for m in range(2)]
        V = al("d_V", [128, NT, 132], BF16)
        NP_ = 4
        Psb = [al(f"d_P{k}", [128, 512], BF16) for k in range(NP_)]
        rrb = [al(f"d_rr{k}", [128, 16], F32) for k in range(2)]
        tmpb = [al(f"d_tmp{k}", [128, 2, 2, 128], F32) for k in range(2)]
        osbb = [al(f"d_o{k}", [128, 2, 128], F32) for k in range(2)]
        sqb = [al(f"d_sq{k}", [128, 2, 128], F32) for k in range(2)]
        t2b = [al(f"d_t2{k}", [128, 2, 128], F32) for k in range(2)]
        nrmb = [al(f"d_n{k}", [128, 2, 128], BF16) for k in range(2)]
        dst = [al(f"d_dst{k}", [128, S], BF16) for k in range(2)]
        lamvec = al("d_lamvec", [128, 2], F32)
        psT7 = ps[7][:, :].bitcast(BF16)
        psbig = self.psbig
        self.memset("pool", Q[1][0:64, :], 0.0)
        self.memset("pool", Kt[1][0:64, :], 0.0)
        self.memset("pool", V[:, :, 128:129], 1.0)
        self.memset("dve", lamvec[:, 0:1], 1.0)
        self.cp("dve", lamvec[:, 1:2], neglam[:, :])
        krows = [(0, 70), (0, 128)]
        LAG = 2
        NPAIR = NT // 2

        def sbank():
            return ps[(0, 1, 6)[self.rot("dsc", 3)]]

        def load_w(hh):
            return self.diff_load_w(win_d, hh)
        Wn = self._diffW0
        for h in range(8):
            wq, wk, wv = Wn
            self.dma("sp", Q[0][64:70, :], qx_d[h], "dx0", writes=[Q[0][64:70, :]])
            self.dma("sp", Kt[0][64:70, :], kx_d[h], "dx1", writes=[Kt[0][64:70, :]])
            self.dma("sp", Q[1][0:6, :], qx_d[h], "dx2", writes=[Q[1][0:6, :]])
            self.dma("sp", Kt[1][0:6, :], kx_d[h], "dx3", writes=[Kt[1][0:6, :]])
            for tb in range(NB):
                tsl = slice(tb * 512, (tb + 1) * 512)
                bank = sbank()
                for kc in range(KC):
                    self.mm(bank[:, :], wq[:, kc, :], hT[:, kc, tsl], kc == 0, kc == KC - 1)
                self.ts("dve", Q[0][0:64, tsl], bank[0:64, :], 0.125, ALU.mult)
                self.ts("dve", Q[1][64:128, tsl], bank[64:128, :], 0.125, ALU.mult)
                bank = sbank()
                for kc in range(KC):
                    self.mm(bank[:, :], wk[:, kc, :], hT[:, kc, tsl], kc == 0, kc == KC - 1)
                self.cp("dve", Kt[0][0:64, tsl], bank[0:64, :])
                self.cp("dve", Kt[1][64:128, tsl], bank[64:128, :])
            for t4 in range(4):
                bank = sbank()
                for j in range(4):
                    t_ = t4 * 4 + j
                    tok = slice(t_ * 128, (t_ + 1) * 128)
                    for kc in range(KC):
                        self.mm(bank[:, j * 128:(j + 1) * 128], hT[:, kc, tok], wv[:, kc, :], kc == 0, kc == KC - 1)
                self.cp("dve", V[:, t4 * 4:t4 * 4 + 4, 0:128], bank[:, :].rearrange("p (a b) -> p a b", a=4))
            if h + 1 < 8:
                Wn = load_w(h + 1)
            if h == 0:
                self._mergeA0 = self.merge_load_a(0)
            if h == 7:
                self._mergeB0 = self.merge_load_b(0)

            def qk_exp(p, ka, m):
                r0, r1 = krows[m]
                qa = 2 * p
                q2 = slice(qa * 128, (qa + 2) * 128)
                sb_ = sbank()
                if ka < qa:
                    for kl in range(2):
                        kt = ka + kl
                        self.mm(sb_[:, kl * 256:(kl + 1) * 256], Kt[m][r0:r1, kt * 128:(kt + 1) * 128],
                                Q[m][r0:r1, q2], True, True)
                    ncol = 512
                    items = [(ka, 0, 0), (ka, 1, 128), (ka + 1, 0, 256), (ka + 1, 1, 384)]
                else:
                    self.mm(sb_[:, 0:256], Kt[m][r0:r1, qa * 128:(qa + 1) * 128], Q[m][r0:r1, q2], True, True,
                            skip=True)
                    self.mm(sb_[:, 0:128], self.idb[:, :], umask[:, :], False, True, skip=True)
                    qb_ = slice((qa + 1) * 128, (qa + 2) * 128)
                    self.mm(sb_[:, 256:384], Kt[m][r0:r1, qb_], Q[m][r0:r1, qb_], True, True, skip=True)
                    self.mm(sb_[:, 256:384], self.idb[:, :], umask[:, :], False, True, skip=True)
                    ncol = 384
                    items = [(qa, 0, 0), (qa, 1, 128), (qa + 1, 1, 256)]
                pb = Psb[self.rot("dP", NP_)]
                self.act(pb[:, 0:ncol], sb_[:, 0:ncol], AF.Exp)
                return (p, ka, m, items, pb)

            def pv(u):
                p, ka, m, items, pb = u
                for (kt, j, c0) in items:
                    qt = 2 * p + j
                    po = ps[2 + (qt % 4)]
                    self.mm(po[:, m * 256:m * 256 + 129], pb[:, c0:c0 + 128], V[:, kt, 0:129],
                            (kt == 0 and m == 0), kt == qt, skip=True)
                if m == 1 and ka == 2 * p:
                    epilogue(p)
                    if p > 0:
                        epi2(p - 1)

            def epi2(p):
                k2_ = p % 2
                for j in range(2):
                    self.tr(psT7[:, j * 128:(j + 1) * 128], nrmb[k2_][:, j, :], self.idb[:, :])
                self.cp("act", dst[h % 2][:, 2 * p * 128:(2 * p + 2) * 128], psT7[:, 0:256])

            def epilogue(p):
                k2_ = p % 2
                b0 = (2 * p) % 4
                pp = psbig[:, b0 * 512:(b0 + 2) * 512]
                pp4 = pp.rearrange("p (b m c) -> p b m c", b=2, m=2)
                r = rrb[k2_]
                r4 = r[:, 0:4].rearrange("p (b m) -> p b m", b=2)
                rl4 = r[:, 4:8].rearrange("p (b m) -> p b m", b=2)
                tmp, ob, sq, t2, nb_ = tmpb[k2_], osbb[k2_], sqb[k2_], t2b[k2_], nrmb[k2_]
                P.op("dve", lambda e: e.reciprocal(out=r4, in_=pp4[:, :, :, 128]), reads=[pp], writes=[r[:, 0:4]])
                self.tt("dve", rl4, r4, lamvec[:, :].unsqueeze(1).to_broadcast([128, 2, 2]), ALU.mult)
                P.op("dve", lambda e: e.tensor_tensor(out=tmp[:, :, :, :], in0=pp4[:, :, :, 0:128],
                                                      in1=rl4.unsqueeze(3).to_broadcast([128, 2, 2, 128]),
                                                      op=ALU.mult),
                     reads=[pp, r[:, 4:8]], writes=[tmp[:, :, :, :]])
                self.tt("dve", ob[:, :, :], tmp[:, :, 0, :], tmp[:, :, 1, :], ALU.add)
                self.tt("dve", sq[:, :, :], ob[:, :, :], ob[:, :, :], ALU.mult)
                P.op("dve", lambda e: e.reduce_sum(out=r[:, 8:10], in_=sq[:, :, :], axis=mybir.AxisListType.X),
                     reads=[sq[:, :, :]], writes=[r[:, 8:10]])
                self.ts("dve", r[:, 10:12], r[:, 8:10], 1.0 / 128, ALU.mult, EPS, ALU.add)
                self.tt("pool", r[:, 12:14], r[:, 10:12], self.nhalf[:, 0:2], ALU.pow)
                self.tt("dve", t2[:, :, :], ob[:, :, :], r[:, 12:14].unsqueeze(2).to_broadcast([128, 2, 128]),
                        ALU.mult)
                self.tt("dve", nb_[:, :, :], t2[:, :, :], sublnB[:, :].unsqueeze(1).to_broadcast([128, 2, 128]),
                        ALU.mult)

            pending = []
            for p in range(NPAIR):
                for ka in range(0, 2 * p + 2, 2):
                    for m in range(2):
                        pending.append(qk_exp(p, ka, m))
                        if len(pending) > LAG:
                            pv(pending.pop(0))
            while pending:
                pv(pending.pop(0))
            epi2(NPAIR - 1)
            self.dma("sp", diff_sp[:, :, h, :].rearrange("b p t -> p b t"),
                     dst[h % 2][:, :].rearrange("p (b t) -> p b t", b=8), f"dsp{h % 2}",
                     reads=[dst[h % 2][:, :]], dwrites=[("diff_sp", h)])

    def merge_load_a(self, qd):
        win_d, wro_d, wdo_d, wout_d = self._mw
        c0 = qd * 256
        return self.wload("A", [(wro_d[:, c0:c0 + 256], 16, 256),
                                (wdo_d[:, c0:c0 + 256], 8, 256),
                                (win_d[:, O_G0 + c0:O_G0 + c0 + 256], 8, 256)])

    def merge_load_b(self, qd):
        win_d, wro_d, wdo_d, wout_d = self._mw
        c0 = qd * 256
        return self.wload("B", [(win_d[:, O_G1 + c0:O_G1 + c0 + 256], 8, 256),
                                (wout_d[c0:c0 + 256, :], 2, D)])

    def merge(self, win_d, wro_d, wdo_d, wout_d, ret_sp, diff_sp, tail=None):
        P, ps, hT, xT = self.P, self.ps, self.hT, self.xT
        A = self.arena
        o = [0]

        def al(name, shape, dtype):
            n = int(np.prod(shape[1:])) * ISZ[dtype]
            t = P.sb(name, shape, dtype, at=A + o[0])
            o[0] += (n + 63) // 64 * 64
            assert o[0] <= self.asz, ("merge arena", o[0], self.asz)
            return t
        TB = 256
        gin = [al(f"m_gin{k}", [128, 16, TB], BF16) for k in range(2)]
        din_ = [al(f"m_din{k}", [128, 8, TB], BF16) for k in range(2)]
        s0 = [al(f"m_s0{k}", [128, TB], F32) for k in range(2)]
        s1 = [al(f"m_s1{k}", [128, TB], F32) for k in range(2)]
        tq = [al(f"m_t{k}", [128, TB], F32) for k in range(2)]
        uq = [al(f"m_u{k}", [128, TB], F32) for k in range(2)]
        yT = [al(f"m_y{k}", [128, 2, TB], BF16) for k in range(2)]
        def load_q(qd):
            return self.merge_load_a(qd) + self.merge_load_b(qd)

        def outproj(wou, tb, lastq=False):
            tsl = slice(tb * TB, (tb + 1) * TB)
            bi = tb % 2
            for dc in range(KC):
                po = ps[self.rot("mps", 8)]
                for yc in range(2):
                    self.mm(po[:, 0:TB], wou[:, yc, dc * 128:(dc + 1) * 128], yT[bi][:, yc, :], yc == 0, yc == 1)
                self.stt("dve", xT[:, dc, tsl], po[:, 0:TB], self.gatef[:, 8 + dc:8 + dc + 1], xT[:, dc, tsl],
                         ALU.mult, ALU.add)
            if lastq and tail is not None and tb % 2 == 1:
                tail(tb // 2)

        Wn = self._mergeA0 + self._mergeB0
        pend = None
        for qd in range(4):
            wro, wdo, wg0, wg1, wou = Wn
            if pend is not None:
                outproj(*pend)
                pend = None
            if qd + 1 < 4:
                Wn = load_q(qd + 1)
            for tb in range(S // TB):
                tsl = slice(tb * TB, (tb + 1) * TB)
                bi = tb % 2
                self.dma("sp", gin[bi][:, :, :], ret_sp[tb], f"mgi{bi}",
                         writes=[gin[bi][:, :, :]], dreads=[("ret_sp", hh, tb) for hh in range(4)])
                self.dma("sp", din_[bi][:, :, :], diff_sp[tb], f"mdi{bi}",
                         writes=[din_[bi][:, :, :]], dreads=[("diff_sp", hh) for hh in range(8)])
                for yc in range(2):
                    ys = slice(yc * 128, (yc + 1) * 128)
                    k = self.rot("mrot", 2)
                    pg0 = ps[self.rot("mps", 8)]
                    for kc in range(8):
                        self.mm(pg0[:, 0:TB], wg0[:, kc, ys], hT[:, kc, tsl], kc == 0, kc == 7)
                    pg1 = ps[self.rot("mps", 8)]
                    for kc in range(8):
                        self.mm(pg1[:, 0:TB], wg1[:, kc, ys], hT[:, kc, tsl], kc == 0, kc == 7)
                    self.act(s0[k][:, :], pg0[:, 0:TB], AF.Sigmoid)
                    self.act(s1[k][:, :], pg1[:, 0:TB], AF.Sigmoid)
                    pr = ps[self.rot("mps", 8)]
                    for kc in range(16):
                        self.mm(pr[:, 0:TB], wro[:, kc, ys], gin[bi][:, kc, :], kc == 0, kc == 15)
                    pd = ps[self.rot("mps", 8)]
                    for kc in range(8):
                        self.mm(pd[:, 0:TB], wdo[:, kc, ys], din_[bi][:, kc, :], kc == 0, kc == 7)
                    self.tt("dve", tq[k][:, :], pr[:, 0:TB], s0[k][:, :], ALU.mult)
                    self.tt("dve", uq[k][:, :], pd[:, 0:TB], s1[k][:, :], ALU.mult)
                    self.tt("dve", yT[bi][:, yc, :], tq[k][:, :], uq[k][:, :], ALU.add)
                if pend is not None:
                    outproj(*pend)
                pend = (wou, tb, qd == 3)
        outproj(*pend)

    def final_out(self, gfin):
        self.P.op("sp", lambda e: e.nop(), dreads=[f"out{t}" for t in range(NT)])

    def final_block(self, tb, r):
        ps, xT = self.ps, self.xT
        tsl = slice(tb * 512, (tb + 1) * 512)
        for dc in range(KC):
            self.stt("dve", self.yf[dc][:, :], xT[:, dc, tsl], self.gfin[:, dc:dc + 1], r[:, :], ALU.mult, ALU.mult)
        for c in range(4):
            t_ = tb * 4 + c
            ost = self.ost[t_ % 2]
            for half in range(2):
                bank = ps[self.rot("pso", 4)]
                for j in range(4):
                    dc = half * 4 + j
                    self.tr(bank[:, j * 128:(j + 1) * 128], self.yf[dc][:, c * 128:(c + 1) * 128], self.id32[:, :])
                self.cp("act" if half == 0 else "dve", ost[:, half * 512:(half + 1) * 512], bank[:, :])
            self.dma("sp", self.out_d[t_ * 128:(t_ + 1) * 128, :], ost[:, :], f"xout{t_ % 2}", reads=[ost[:, :]],
                     dwrites=[f"out{t_}"])


def _bf(a):
    return a.astype(ml_dtypes.bfloat16).astype(np.float64)


def host_consts():
    c = {}
    c["c_ident"] = np.eye(128, dtype=np.float32)
    pos = np.arange(128, dtype=np.float64)
    kdec = np.zeros((4, 128, 512), np.float32)
    xi = np.zeros((128, 4), np.float32)
    for h in range(4):
        log_g = np.log1p(-np.exp2(-5.0 - h))
        row = np.exp(-(pos + 1.0) * log_g) / 16.0
        kdec[h] = np.tile(row, 4)[None, :].astype(np.float32)
        xi[:, h] = np.exp((pos + 1.0) * log_g).astype(np.float32)
    c["c_kdec"] = kdec
    c["c_xi"] = xi
    r = np.arange(128)
    c["c_mask01"] = (r[None, :] >= r[:, None]).astype(np.float32)
    c["c_umask"] = np.where(r[None, :] < r[:, None], NEG, 0.0).astype(np.float32)
    p = np.arange(S, dtype=np.float64)
    qx = np.zeros((8, 6, S), np.float32)
    kx = np.zeros((8, 6, S), np.float32)
    for h in range(8):
        slope = 2.0 ** (-8.0 * (h + 1.0) / 8.0)
        a = slope * p
        a1 = _bf(a)
        a2 = _bf(a - a1)
        a3 = _bf(a - a1 - a2)
        qx[h, 0], qx[h, 1], qx[h, 2] = -a1, -a2, -a3
        qx[h, 3:6] = 1.0
        kx[h, 0:3] = 1.0
        kx[h, 3], kx[h, 4], kx[h, 5] = a1, a2, a3
    c["c_qx"] = qx.astype(ml_dtypes.bfloat16)
    c["c_kx"] = kx.astype(ml_dtypes.bfloat16)
    return c


def pvec(v, n):
    return np.ascontiguousarray(np.asarray(v, np.float32).reshape(n, 128).T)


def make_in_maps(inputs):
    f = lambda k: np.ascontiguousarray(np.asarray(inputs[k], np.float32))
    shared = dict(
        w_cond=f("w_cond")[0], b_cond=pvec(f("b_cond")[0], 72),
        g_norm=np.ascontiguousarray(np.concatenate([pvec(f("g_norm")[0, i], 8) for i in range(3)], axis=1)),
        w_ffn1_in=f("w_ffn1_in")[0], w_ffn1_out=f("w_ffn1_out")[0], w_in=f("w_in")[0],
        w_ret_out=f("w_ret_out")[0], diff_lambda=f("diff_lambda")[0].reshape(256),
        diff_subln=f("diff_subln")[0].reshape(128), w_diff_out=f("w_diff_out")[0], w_out=f("w_out")[0],
        w_ffn2_in=f("w_ffn2_in")[0], w_ffn2_out=f("w_ffn2_out")[0], g_final=pvec(f("g_final"), 8),
    )
    shared.update(host_consts())
    x = f("x")
    c = f("c")
    maps = []
    for b in range(8):
        m = dict(shared)
        m["x"] = np.ascontiguousarray(x[b])
        m["c"] = pvec(c[b], 8)
        maps.append(m)
    return maps


def build_nc(stage=None):
    nc = bass.Bass("TRN2", target_bir_lowering=False)
    k = K(nc, stage)
    P = k.build()
    return nc, P


def kernel(**inputs):
    nc, P = build_nc(None)
    maps = make_in_maps(inputs)
    res = run_bass_kernel_spmd(nc, maps, core_ids=list(range(8)))
    out = np.stack([np.asarray(r["out"], np.float32) for r in res.results], axis=0)
    return out
```
